# Optimizing a Trainium2 kernel written in Bass

```python
import jax, jax.numpy as jnp
from jax import lax
import numpy as np

D_MODEL = 1024
BATCH = 16
SEQ = 2048
DEPTH = 2

GRID_W = 64
CTX_LEN = 256
N_MIXERS = 2
HEAD_DIM = 64
DA_HEADS = D_MODEL // (2 * HEAD_DIM)
NA_HEADS = D_MODEL // HEAD_DIM
NA_WIN_ROWS = 8
NA_WIN_COLS = 16
D_FF = ((8 * D_MODEL + 3 * 256 - 1) // (3 * 256)) * 256
ROPE_THETA = 10000.0
Q_BLOCK = 128
EPS = 1e-6

kernel_name = "hybrid_diffattn_natten_prefix_dit"


def rmsnorm(x, g):
    xf = x.astype(jnp.float32)
    y = xf * lax.rsqrt(jnp.mean(xf * xf, axis=-1, keepdims=True) + EPS)
    return (y * g.astype(jnp.float32)).astype(x.dtype)


def adaln(cond, w, b):
    mod = jax.nn.silu(cond) @ w + b
    return jnp.split(mod, 6, axis=-1)


def modulate(xn, shift, scale):
    return xn * (1 + scale) + shift


def swiglu(h, w_gate_up, w_down):
    gu = h @ w_gate_up
    g, u = jnp.split(gu, 2, axis=-1)
    return (jax.nn.silu(g) * u) @ w_down


def axial_rope_tables(n):
    t = jnp.arange(n)
    row = (t // GRID_W).astype(jnp.float32)
    col = (t % GRID_W).astype(jnp.float32)
    half = HEAD_DIM // 2
    freqs = 1.0 / (ROPE_THETA ** (jnp.arange(0, half, 2, dtype=jnp.float32) / half))
    ar = row[:, None] * freqs
    ac = col[:, None] * freqs
    ang = jnp.concatenate([ar, ar, ac, ac], axis=-1)
    return jnp.cos(ang), jnp.sin(ang)


def rotate_half(x):
    x1, x2 = jnp.split(x, 2, axis=-1)
    return jnp.concatenate([-x2, x1], axis=-1)


def apply_rope(t, cos, sin):
    half = t.shape[-1] // 2
    rot = jnp.concatenate([rotate_half(t[..., :half]), rotate_half(t[..., half:])], axis=-1)
    shape = (t.shape[1],) + (1,) * (t.ndim - 3) + (t.shape[-1],)
    return t * cos.reshape(shape).astype(t.dtype) + rot * sin.reshape(shape).astype(t.dtype)


def diff_attention(h_lat, h_ctx, wqkv, lq1, lk1, lq2, lk2, subln, wo, lam_init, need_ctx_out):
    B, N, _ = h_lat.shape
    L = h_ctx.shape[1]
    scale = HEAD_DIM ** -0.5

    def proj(h):
        n = h.shape[1]
        q, k, v = jnp.split(h @ wqkv, 3, axis=-1)
        return (q.reshape(B, n, DA_HEADS, 2, HEAD_DIM),
                k.reshape(B, n, DA_HEADS, 2, HEAD_DIM),
                v.reshape(B, n, DA_HEADS, 2 * HEAD_DIM))

    q_l, k_l, v_l = proj(h_lat)
    q_c, k_c, v_c = proj(h_ctx)
    cos, sin = axial_rope_tables(N)
    q_l = apply_rope(q_l, cos, sin)
    k_l = apply_rope(k_l, cos, sin)

    f32 = jnp.float32
    lam = (jnp.exp(jnp.sum(lq1.astype(f32) * lk1.astype(f32)))
           - jnp.exp(jnp.sum(lq2.astype(f32) * lk2.astype(f32))) + lam_init)

    def attend(q_blk, k, v):
        s = jnp.einsum('bqhmd,bkhmd->bhmqk', q_blk, k).astype(f32) * scale
        p = jax.nn.softmax(s, axis=-1)
        p = p[:, :, 0] - lam * p[:, :, 1]
        return jnp.einsum('bhqk,bkhe->bqhe', p.astype(v.dtype), v)

    k_all = jnp.concatenate([k_c, k_l], axis=1)
    v_all = jnp.concatenate([v_c, v_l], axis=1)
    nblk = N // Q_BLOCK
    q_blocks = q_l.reshape(B, nblk, Q_BLOCK, DA_HEADS, 2, HEAD_DIM).transpose(1, 0, 2, 3, 4, 5)
    o_l = lax.map(lambda qb: attend(qb, k_all, v_all), q_blocks)
    o_l = o_l.transpose(1, 0, 2, 3, 4).reshape(B, N, DA_HEADS, 2 * HEAD_DIM)

    def finish(o, n):
        o = rmsnorm(o, subln) * (1.0 - lam_init)
        return o.reshape(B, n, D_MODEL) @ wo

    y_l = finish(o_l, N)
    y_c = finish(attend(q_c, k_c, v_c), L) if need_ctx_out else None
    return y_l, y_c


def neighbourhood_attention(h_lat, h_ctx, wqkv, rpb, wo, need_ctx_out):
    B, N, _ = h_lat.shape
    L = h_ctx.shape[1]
    rows = N // GRID_W
    wr = min(NA_WIN_ROWS, rows)
    scale = HEAD_DIM ** -0.5
    f32 = jnp.float32

    def proj(h):
        n = h.shape[1]
        q, k, v = jnp.split(h @ wqkv, 3, axis=-1)
        return tuple(t.reshape(B, n, NA_HEADS, HEAD_DIM) for t in (q, k, v))

    q_l, k_l, v_l = proj(h_lat)
    q_c, k_c, v_c = proj(h_ctx)
    kg = k_l.reshape(B, rows, GRID_W, NA_HEADS, HEAD_DIM)
    vg = v_l.reshape(B, rows, GRID_W, NA_HEADS, HEAD_DIM)
    qg = q_l.reshape(B, rows, GRID_W, NA_HEADS, HEAD_DIM).transpose(1, 0, 2, 3, 4)

    qc = np.arange(GRID_W)
    col_start = np.clip(qc - NA_WIN_COLS // 2, 0, GRID_W - NA_WIN_COLS)
    col_mask = (qc[None, :] >= col_start[:, None]) & (qc[None, :] < col_start[:, None] + NA_WIN_COLS)
    dc_idx = np.clip(qc[None, :] - qc[:, None] + NA_WIN_COLS - 1, 0, 2 * NA_WIN_COLS - 2)
    rpb_cols = rpb[:, :, dc_idx]
    col_mask = jnp.asarray(col_mask)[None, None, :, None, :]

    def row_block(args):
        q_row, r = args
        rs = jnp.clip(r - NA_WIN_ROWS // 2, 0, rows - wr)
        kb = lax.dynamic_slice_in_dim(kg, rs, wr, axis=1)
        vb = lax.dynamic_slice_in_dim(vg, rs, wr, axis=1)
        dr_idx = rs + jnp.arange(wr) - r + NA_WIN_ROWS - 1
        bias = rpb_cols[:, dr_idx].transpose(0, 2, 1, 3)
        s_band = jnp.einsum('bqhd,brkhd->bhqrk', q_row, kb).astype(f32) * scale + bias.astype(f32)
        s_band = jnp.where(col_mask, s_band, -jnp.inf)
        s_ctx = jnp.einsum('bqhd,bchd->bhqc', q_row, k_c).astype(f32) * scale
        s = jnp.concatenate([s_band.reshape(B, NA_HEADS, GRID_W, wr * GRID_W), s_ctx], axis=-1)
        p = jax.nn.softmax(s, axis=-1).astype(v_c.dtype)
        p_band = p[..., :wr * GRID_W].reshape(B, NA_HEADS, GRID_W, wr, GRID_W)
        p_ctx = p[..., wr * GRID_W:]
        return (jnp.einsum('bhqrk,brkhd->bqhd', p_band, vb)
                + jnp.einsum('bhqc,bchd->bqhd', p_ctx, v_c))

    o_l = lax.map(row_block, (qg, jnp.arange(rows)))
    y_l = o_l.transpose(1, 0, 2, 3, 4).reshape(B, N, D_MODEL) @ wo

    y_c = None
    if need_ctx_out:
        s = jnp.einsum('bqhd,bkhd->bhqk', q_c, k_c).astype(f32) * scale
        p = jax.nn.softmax(s, axis=-1).astype(v_c.dtype)
        y_c = jnp.einsum('bhqk,bkhd->bqhd', p, v_c).reshape(B, L, D_MODEL) @ wo
    return y_l, y_c


def setup_inputs(seed: int = 0) -> dict:
    key = jax.random.key(seed)
    ks = jax.random.split(key, 24)
    D, F = D_MODEL, D_FF
    n_a = (DEPTH + 1) // 2
    n_b = DEPTH // 2
    nrm = jax.random.normal
    f32 = jnp.float32
    return {
        "x": nrm(ks[0], (BATCH, SEQ, D), f32),
        "c": nrm(ks[1], (BATCH, D), f32),
        "ctx": nrm(ks[2], (BATCH, CTX_LEN, D), f32),
        "c_ctx": nrm(ks[3], (D,), f32),
        "ada_w": nrm(ks[4], (DEPTH, D, 6 * D), f32) * D ** -0.5,
        "ada_b": nrm(ks[5], (DEPTH, 6 * D), f32) * 0.01,
        "norm_mix": 1.0 + 0.02 * nrm(ks[6], (DEPTH, D), f32),
        "norm_ffn": 1.0 + 0.02 * nrm(ks[7], (DEPTH, D), f32),
        "da_wqkv": nrm(ks[8], (n_a, D, 3 * D), f32) * D ** -0.5,
        "da_lambda_q1": nrm(ks[9], (n_a, HEAD_DIM), f32) * 0.1,
        "da_lambda_k1": nrm(ks[10], (n_a, HEAD_DIM), f32) * 0.1,
        "da_lambda_q2": nrm(ks[11], (n_a, HEAD_DIM), f32) * 0.1,
        "da_lambda_k2": nrm(ks[12], (n_a, HEAD_DIM), f32) * 0.1,
        "da_subln": 1.0 + 0.02 * nrm(ks[13], (n_a, 2 * HEAD_DIM), f32),
        "da_wo": nrm(ks[14], (n_a, D, D), f32) * D ** -0.5,
        "na_wqkv": nrm(ks[15], (n_b, D, 3 * D), f32) * D ** -0.5,
        "na_rpb": nrm(ks[16], (n_b, NA_HEADS, 2 * NA_WIN_ROWS - 1, 2 * NA_WIN_COLS - 1), f32) * 0.02,
        "na_wo": nrm(ks[17], (n_b, D, D), f32) * D ** -0.5,
        "ffn_w_gate_up": nrm(ks[18], (DEPTH, D, 2 * F), f32) * D ** -0.5,
        "ffn_w_down": nrm(ks[19], (DEPTH, F, D), f32) * F ** -0.5,
        "norm_final": 1.0 + 0.02 * nrm(ks[20], (D,), f32),
    }


def reference(x, c, ctx, c_ctx, ada_w, ada_b, norm_mix, norm_ffn,
              da_wqkv, da_lambda_q1, da_lambda_k1, da_lambda_q2, da_lambda_k2, da_subln, da_wo,
              na_wqkv, na_rpb, na_wo, ffn_w_gate_up, ffn_w_down, norm_final):
    h, hc = x, ctx
    for i in range(DEPTH):
        last = i == DEPTH - 1
        sh_m, sc_m, g_m, sh_f, sc_f, g_f = (t[:, None, :] for t in adaln(c, ada_w[i], ada_b[i]))
        csh_m, csc_m, cg_m, csh_f, csc_f, cg_f = adaln(c_ctx, ada_w[i], ada_b[i])

        xn = modulate(rmsnorm(h, norm_mix[i]), sh_m, sc_m)
        xc = modulate(rmsnorm(hc, norm_mix[i]), csh_m, csc_m)
        j = i // N_MIXERS
        if i % N_MIXERS == 0:
            lam_init = 0.8 - 0.6 * float(np.exp(-0.3 * i))
            y, yc = diff_attention(xn, xc, da_wqkv[j], da_lambda_q1[j], da_lambda_k1[j],
                                   da_lambda_q2[j], da_lambda_k2[j], da_subln[j], da_wo[j],
                                   lam_init, not last)
        else:
            y, yc = neighbourhood_attention(xn, xc, na_wqkv[j], na_rpb[j], na_wo[j], not last)

        h = h + g_m * y
        h = h + g_f * swiglu(modulate(rmsnorm(h, norm_ffn[i]), sh_f, sc_f),
                             ffn_w_gate_up[i], ffn_w_down[i])
        if not last:
            hc = hc + cg_m * yc
            hc = hc + cg_f * swiglu(modulate(rmsnorm(hc, norm_ffn[i]), csh_f, csc_f),
                                    ffn_w_gate_up[i], ffn_w_down[i])
    return rmsnorm(h, norm_final)
```

```python
import contextlib
import numpy as np
import concourse.bass as bass
import concourse.mybir as mybir
from concourse.bass_utils import run_bass_kernel_spmd

F32 = mybir.dt.float32
BF16 = mybir.dt.bfloat16
U8 = mybir.dt.uint8
AF = mybir.ActivationFunctionType
ALU = mybir.AluOpType
AX = mybir.AxisListType

ENGS = ['pe', 'act', 'dve', 'pool', 'sp']

D = 1024
NTOK = 2304
LCTX = 256
NLAT = 2048
DFF = 2816
NJ = DFF // 128
EPS = 1e-6
NCORES = 8
NB = 2
import os
ROPE_ENG = os.environ.get('ROPE_ENG', 'dve')


import types


def freeze(fn):
    if getattr(fn, '__closure__', None) is None:
        return fn
    cells = []
    for c in fn.__closure__:
        try:
            v = c.cell_contents
            if isinstance(v, types.FunctionType) and v is not fn:
                v = freeze(v)
            cells.append(types.CellType(v))
        except ValueError:
            cells.append(c)
    g = types.FunctionType(fn.__code__, fn.__globals__, fn.__name__, fn.__defaults__, tuple(cells))
    g.__kwdefaults__ = fn.__kwdefaults__
    return g


class _Op:
    __slots__ = ('id', 'eng', 'fn', 'chan', 'seq', 'waits', 'signal', 'clock', 'is_dma')


class Sched:
    def __init__(self, nc, dma_slots=None):
        self.nc = nc
        self.ops = []
        self.eng_ops = {e: [] for e in ENGS}
        self.units = {}
        self.eng_clock = {e: {} for e in ENGS}
        self.comp_seq = {e: 0 for e in ENGS}
        self.chan_ops = {}
        self.dma_slots = dma_slots or {'sp': 8, 'pool': 10, 'act': 4}
        self.dma_rr = {q: 0 for q in self.dma_slots}
        self.dma_cnt = {}
        self.dma_last = {}

    def op(self, eng, fn, reads=(), writes=(), dma=False):
        o = _Op()
        o.id = len(self.ops)
        o.eng = eng
        o.fn = freeze(fn)
        o.is_dma = dma
        o.signal = False
        deps = set()
        ps_reads = [u for u in reads if u[0] == 'ps']
        if ps_reads:
            reads = [u for u in reads if u[0] != 'ps']
            writes = list(writes) + [u for u in ps_reads if u not in writes]
        for u in reads:
            st = self.units.get(u)
            if st is not None and st[0] is not None:
                deps.add(st[0])
        for u in writes:
            st = self.units.get(u)
            if st is not None:
                if st[0] is not None:
                    deps.add(st[0])
                deps.update(st[1])
        if dma:
            q = eng
            slot = self.dma_rr[q]
            self.dma_rr[q] = (slot + 1) % self.dma_slots[q]
            chan = ('dma', q, slot)
            prev = self.dma_last.get(chan)
            if prev is not None:
                deps.add(prev)
            self.dma_cnt[chan] = self.dma_cnt.get(chan, 0) + 1
            o.chan = chan
            o.seq = self.dma_cnt[chan]
            self.dma_last[chan] = o.id
        else:
            self.comp_seq[eng] += 1
            o.chan = eng
            o.seq = self.comp_seq[eng]
        clk = self.eng_clock[eng]
        waits = {}
        for d in sorted(deps, reverse=True):
            dop = self.ops[d]
            if dop.chan == 'pe' and eng == 'pe' and not dma:
                continue
            if clk.get(dop.chan, 0) >= dop.seq:
                continue
            if waits.get(dop.chan, 0) < dop.seq:
                waits[dop.chan] = dop.seq
            for c, s in dop.clock.items():
                if clk.get(c, 0) < s:
                    clk[c] = s
        o.waits = list(waits.items())
        for c, s in o.waits:
            self.chan_ops[c][s - 1].signal = True
        o.clock = dict(clk)
        o.clock[o.chan] = o.seq
        self.chan_ops.setdefault(o.chan, []).append(o)
        self.ops.append(o)
        self.eng_ops[eng].append(o)
        for u in reads:
            st = self.units.get(u)
            if st is None:
                st = [None, []]
                self.units[u] = st
            st[1].append(o.id)
        for u in writes:
            self.units[u] = [o.id, []]
        return o

    def emit(self):
        nc = self.nc
        chans = list(self.chan_ops.keys())
        with contextlib.ExitStack() as es:
            sems = {}
            for c in chans:
                nm = 's_' + ('_'.join(str(x) for x in c) if isinstance(c, tuple) else c)
                sems[c] = es.enter_context(nc.semaphore(nm))
            sigcount = {}
            for c, lst in self.chan_ops.items():
                if isinstance(c, tuple):
                    continue
                n = 0
                arr = []
                for o in lst:
                    if o.signal:
                        n += 1
                    arr.append(n)
                sigcount[c] = arr

            def wval(c, s):
                if isinstance(c, tuple):
                    return 16 * s
                return sigcount[c][s - 1]

            block = es.enter_context(nc.Block())
            engmap = {'pe': block.tensor, 'act': block.scalar, 'dve': block.vector,
                      'pool': block.gpsimd, 'sp': block.sync}
            for e in ENGS:
                lst = self.eng_ops[e]
                if not lst:
                    continue
                fin = [(c, self.dma_cnt[c]) for c in chans if isinstance(c, tuple) and c[1] == e]

                def body(engine, lst=lst, fin=fin):
                    for o in lst:
                        for c, s in o.waits:
                            engine.wait_ge(sems[c], wval(c, s))
                        ins = o.fn(engine)
                        if o.is_dma:
                            ins.then_inc(sems[o.chan], 16)
                        elif o.signal:
                            ins.then_inc(sems[o.chan], 1)
                    for c, n in fin:
                        engine.wait_ge(sems[c], 16 * n)
                engmap[e](body)


def _rope_tables():
    t = np.arange(NLAT)
    row = (t // 64).astype(np.float32)
    col = (t % 64).astype(np.float32)
    half = 32
    freqs = (1.0 / (np.float32(10000.0) ** (np.arange(0, half, 2, dtype=np.float32) / np.float32(half)))).astype(np.float32)
    ar = row[:, None] * freqs
    ac = col[:, None] * freqs
    ang = np.concatenate([ar, ar, ac, ac], axis=-1).astype(np.float32)
    cos = np.cos(ang).astype(np.float32).T
    sin = np.sin(ang).astype(np.float32).T
    cs = np.stack([np.concatenate([cos, cos], 0), np.concatenate([sin, sin], 0)], axis=1)
    return np.ascontiguousarray(cs)


def _rot_matrix():
    R = np.zeros((128, 128), np.float32)
    for m in range(128):
        d = m % 32
        base = m - d
        if d < 16:
            R[base + d + 16, m] = -1.0
        else:
            R[base + d - 16, m] = 1.0
    return R


def _na_colmask_rev():
    qc = np.arange(64)
    cs = np.clip(qc - 8, 0, 48)
    m = np.zeros((64, 64), np.float32)
    for c in range(64):
        m[cs[c]:cs[c] + 16, c] = 1.0
    mrev = m[:, ::-1]
    return np.ascontiguousarray(np.concatenate([mrev, mrev], 0))


def _na_groups():
    groups = []
    def chunks(lo_row, hi_row):
        return list(range(lo_row // 2, hi_row // 2 + 1))
    groups.append((0, 4, 'full', chunks(0, 7)))
    groups.append((4, 4, 'int', chunks(0, 10)))
    groups.append((8, 8, 'int', chunks(4, 18)))
    groups.append((16, 8, 'int', chunks(12, 26)))
    groups.append((24, 5, 'int', chunks(20, 31)))
    groups.append((29, 3, 'full', chunks(24, 31)))
    return groups


class _Stop(Exception):
    pass


def build_program(nb=NB, layers=(0, 1), first=True, final=True, dbg=False, stop=None):
    def chk(stage):
        if stop == stage:
            raise _Stop()
    nc = bass.Bass("TRN2", target_bir_lowering=False)

    def din(name, shape, dt=F32):
        return nc.dram_tensor(name, list(shape), dt, kind="ExternalInput").ap()

    xT = din("xT", [nb, D, NTOK])
    cT = din("cT", [128, 8, 3])
    ada_r = din("ada_r", [2, 6, 8, 128, 1024])
    ada_bT = din("ada_bT", [128, 2, 48])
    nrm = din("nrm", [128, 5, 8])
    wqkv_r = din("wqkv_r", [2, 3, 8, 128, 1024])
    wo_r = din("wo_r", [2, 8, 128, 1024])
    w1_r = din("w1_r", [2, NJ, 128, 2048])
    w2_r = din("w2_r", [2, NJ, 128, 1024])
    lamv = din("lamv", [1, 256])
    subln = din("subln", [128, 1])
    rpb = din("rpb", [120, 2, 31])
    ropet = din("ropet", [128, 2, NLAT])
    rotm_d = din("rotm", [128, 128])
    cmask_d = din("cmask", [128, 64])
    outT = nc.dram_tensor("outT", [nb, D, NLAT if final else NTOK], F32, kind="ExternalOutput").ap()
    zscr_t = nc.dram_tensor("zscr", [240, 127], BF16)
    zscr = zscr_t.ap()

    S = Sched(nc)
    es = contextlib.ExitStack()
    with es:
        ARENA = 212000
        arena = nc.alloc_sbuf_tensor("arena", [128, ARENA], U8)
        off = [0]

        def carve(nbytes, dt, pattern=None, **kw):
            a = off[0]
            off[0] += (nbytes + 31) // 32 * 32
            assert off[0] <= ARENA, off[0]
            v = arena[:, a:a + nbytes].bitcast(dt)
            if pattern:
                v = v.rearrange(pattern, **kw)
            return v

        hT = carve(8 * NTOK * 4, F32, "p (c t) -> p c t", c=8)
        xn = carve(8 * NTOK * 2, BF16, "p (c t) -> p c t", c=8)
        big = [carve(NTOK * 2, BF16) for _ in range(4)]
        vtok = carve(18 * 128 * 2, BF16, "p (t e) -> p t e", t=18)
        oT = carve(NTOK * 2, BF16)
        NPT = 6
        PT = [carve(512 * 2, BF16) for _ in range(NPT)]
        NW = 12
        wring = carve(NW * 2048, BF16)
        wr = [wring[:, i * 1024:(i + 1) * 1024] for i in range(NW)]
        rstd_buf = [carve(2048, F32) for _ in range(2)]
        lamtmp = carve(160 * 4, F32)
        tab = carve(16384, U8)
        NTMP = 8
        tmp = [carve(2048, U8) for _ in range(NTMP)]
        ones_bf = carve(128 * 2, BF16)
        rotm = carve(128 * 2, BF16)
        ones_f = carve(128 * 4, F32)
        modT = carve(2 * 48 * 3 * 4, F32, "p (l c j) -> p l c j", l=2, c=48)
        Gv = carve(2 * 2 * 3 * 8 * 4, F32, "p (l m j c) -> p l m j c", l=2, m=2, j=3)
        nrm_sb = carve(5 * 8 * 4, F32, "p (n c) -> p n c", n=5)
        adab_sb = carve(2 * 48 * 4, F32, "p (l c) -> p l c", l=2)
        c_sb = carve(8 * 3 * 4, F32, "p (k j) -> p k j", k=8)
        sc_bf = carve(8 * 3 * 2, BF16, "p (k j) -> p k j", k=8)
        small = carve(64 * 4, F32)
        lam_sb = carve(256 * 4, F32)
        subln_sb = carve(4, F32)
        eps_sb = carve(4, F32)
        mx = carve(2 * 2 * 5 * 4, F32, "p (a m b) -> p a m b", a=2, m=2)
        cmask = carve(64 * 2, BF16)
        ps_t = nc.alloc_psum_tensor("ps", [128, 8, 512], F32)
        psb = [ps_t[:, k, :] for k in range(8)]

        cosT = tab[:, 0:8192].bitcast(F32)
        sinT = tab[:, 8192:16384].bitcast(F32)
        na_stg = tab[:, 0:2816].bitcast(BF16).rearrange("p (u c) -> p u c", u=22)
        na_tab = [[tab[:, 2816 + (s * 2 + v) * 2816: 2816 + (s * 2 + v + 1) * 2816].bitcast(BF16)
                   .rearrange("p (u c) -> p u c", u=22) for v in range(2)] for s in range(2)]

        def tmpv(i, dt, w=512):
            nbytes = w * (4 if dt == F32 else 2)
            return tmp[i][:, 0:nbytes].bitcast(dt)

        st = {'tmp': 0, 'pt': 0, 'w': 0, 'wk': 0, 'rs': 0}

        def tmp_next():
            i = st['tmp']
            st['tmp'] = (i + 1) % NTMP
            return i

        def pt_next():
            i = st['pt']
            st['pt'] = (i + 1) % NPT
            return i

        def w_alloc(n):
            p = st['w']
            if p + n > NW:
                p = 0
            st['w'] = p + n
            return list(range(p, p + n))

        WORK = [0, 1, 2, 3]

        def work_next():
            i = st['wk']
            st['wk'] = (i + 1) % len(WORK)
            return WORK[i]

        def U(*a):
            return a

        BLK = [(0, 256)] + [(256 + 512 * i, 512) for i in range(4)]

        S.op('pool', lambda e: e.memset(ones_bf[:], 1.0), writes=[U('ones')])
        S.op('pool', lambda e: e.memset(ones_f[:], 1.0), writes=[U('onesf')])
        S.op('pool', lambda e: e.memset(eps_sb[:], EPS), writes=[U('eps')])
        S.op('pool', lambda e: e.dma_start(out=rotm[:], in_=rotm_d), writes=[U('rotm')], dma=True)
        S.op('pool', lambda e: e.dma_start(out=cmask[:], in_=cmask_d), writes=[U('cmask')], dma=True)
        S.op('sp', lambda e: e.dma_start(out=nrm_sb[:], in_=nrm), writes=[U('nrm')], dma=True)
        S.op('sp', lambda e: e.dma_start(out=adab_sb[:], in_=ada_bT), writes=[U('adab')], dma=True)
        S.op('sp', lambda e: e.dma_start(out=c_sb[:], in_=cT), writes=[U('c')], dma=True)
        S.op('sp', lambda e: e.dma_start(out=lam_sb[0:1, :], in_=lamv), writes=[U('lamrow')], dma=True)
        S.op('sp', lambda e: e.dma_start(out=subln_sb[:], in_=subln), writes=[U('subln')], dma=True)

        S.op('act', lambda e: e.activation(sc_bf[:], c_sb[:], AF.Silu), reads=[U('c')], writes=[U('sc')])

        if first:
            for l in layers:
                for grp in range(6):
                    slots = []
                    for kc in range(8):
                        s = w_alloc(1)[0]
                        slots.append(s)
                        S.op('pool', lambda e, s=s, l=l, grp=grp, kc=kc: e.dma_start(out=wr[s][:], in_=ada_r[l, grp, kc]),
                             writes=[U('w', s)], dma=True)
                        for o in range(8):
                            S.op('pe', lambda e, s=s, o=o, kc=kc: e.matmul(psb[o][:, 0:3], wr[s][:, o * 128:(o + 1) * 128],
                                                                            sc_bf[:, kc, :], start=(kc == 0), stop=(kc == 7)),
                                 reads=[U('w', s), U('sc')], writes=[U('ps', o)])
                    for o in range(8):
                        ch = grp * 8 + o
                        S.op('dve', lambda e, o=o, l=l, ch=ch: e.tensor_scalar(modT[:, l, ch, :], psb[o][:, 0:3],
                                                                                   adab_sb[:, l, ch:ch + 1], None, ALU.add),
                             reads=[U('ps', o), U('adab')], writes=[U('modT')])
            for l in layers:
                for mf in range(2):
                    for j in range(3):
                        sc_base = 8 if mf == 0 else 32
                        nidx = l if mf == 0 else 2 + l
                        S.op('dve', lambda e, l=l, mf=mf, j=j, sc_base=sc_base, nidx=nidx: e.scalar_tensor_tensor(
                            Gv[:, l, mf, j, :], modT[:, l, sc_base:sc_base + 8, j], 1.0, nrm_sb[:, nidx, :], ALU.add, ALU.mult),
                            reads=[U('modT'), U('nrm')], writes=[U('Gv')])

        LAM_INIT = 0.8 - 0.6 * float(np.exp(-0.3 * 0))
        negl = small[:, 0:1]
        sublnS = small[:, 1:2]
        if 0 in layers:
            lr = lam_sb[0:1, :]
            prod = lamtmp
            S.op('dve', lambda e: e.tensor_tensor(prod[0:1, 0:64], lr[:, 0:64], lr[:, 64:128], ALU.mult),
                 reads=[U('lamrow')], writes=[U('lamtmp')])
            S.op('dve', lambda e: e.tensor_tensor(prod[0:1, 64:128], lr[:, 128:192], lr[:, 192:256], ALU.mult),
                 reads=[U('lamrow'), U('lamtmp')], writes=[U('lamtmp')])
            S.op('dve', lambda e: e.tensor_reduce(prod[0:1, 128:130], prod[0:1, 0:128].rearrange("p (a b) -> p a b", a=2),
                                                  AX.X, ALU.add), reads=[U('lamtmp')], writes=[U('lamtmp')])
            S.op('act', lambda e: e.activation(prod[0:1, 130:132], prod[0:1, 128:130], AF.Exp),
                 reads=[U('lamtmp')], writes=[U('lamtmp')])
            S.op('dve', lambda e: e.tensor_tensor(prod[0:1, 132:133], prod[0:1, 131:132], prod[0:1, 130:131], ALU.subtract),
                 reads=[U('lamtmp')], writes=[U('lamtmp')])
            S.op('dve', lambda e: e.tensor_scalar(prod[0:1, 133:134], prod[0:1, 132:133], -LAM_INIT, None, ALU.add),
                 reads=[U('lamtmp')], writes=[U('lamtmp')])
            S.op('pe', lambda e: e.matmul(psb[0][:, 0:1], ones_f[0:1, :], prod[0:1, 133:134], start=True, stop=True),
                 reads=[U('lamtmp'), U('onesf')], writes=[U('ps', 0)])
            S.op('dve', lambda e: e.tensor_copy(negl, psb[0][:, 0:1]), reads=[U('ps', 0)], writes=[U('negl')])
            S.op('dve', lambda e: e.tensor_scalar(sublnS, subln_sb[:], 1.0 - LAM_INIT, None, ALU.mult),
                 reads=[U('subln')], writes=[U('sublnS')])

        if 1 in layers:
            zt = tmpv(1, BF16, 512)
            rp = tmpv(2, F32, 512)
            zv = zt[0:120, 0:254].rearrange("p (a s) -> p a s", a=2)
            S.op('sp', lambda e: e.dma_start(out=rp[0:120, 0:62].rearrange("p (a s) -> p a s", a=2), in_=rpb),
                 writes=[U('tmp', 2)], dma=True)
            S.op('pool', lambda e: e.memset(zt[0:120, 0:254], 0.0), writes=[U('tmp', 1)])
            S.op('act', lambda e: e.activation(zv[:, :, 48:79], rp[0:120, 0:62].rearrange("p (a s) -> p a s", a=2), AF.Exp),
                 reads=[U('tmp', 2), U('tmp', 1)], writes=[U('tmp', 1)])
            S.op('sp', lambda e: e.dma_start(out=zscr.rearrange("(p a) s -> p a s", a=2), in_=zv),
                 reads=[U('tmp', 1)], writes=[U('zscr')], dma=True)

        def rms_rstd(src_blocks, blk, inv_n, nchunk_parts=128):
            t0, w = BLK[blk]
            bank = work_next()
            for c in range(8):
                ti = tmp_next()
                sq = tmpv(ti, BF16)
                S.op('act', lambda e, sq=sq, c=c: e.activation(sq[:, :w], hT[:, c, t0:t0 + w], AF.Square),
                     reads=[U('h', c, blk)], writes=[U('tmp', ti)])
                S.op('pe', lambda e, sq=sq, c=c: e.matmul(psb[bank][:, :w], ones_bf[:], sq[:, :w], start=(c == 0), stop=(c == 7)),
                     reads=[U('tmp', ti), U('ones')], writes=[U('ps', bank)])
            ri = st['rs']
            st['rs'] = 1 - ri
            r = rstd_buf[ri]
            S.op('act', lambda e: e.activation(r[:, :w], psb[bank][:, :w], AF.Ln, bias=eps_sb[:], scale=inv_n),
                 reads=[U('ps', bank), U('eps')], writes=[U('rstd', ri)])
            S.op('act', lambda e: e.activation(r[:, :w], r[:, :w], AF.Exp, scale=-0.5),
                 reads=[U('rstd', ri)], writes=[U('rstd', ri)])
            return ri

        def norm_modulate(l, mf, b, blocks):
            for blk in blocks:
                t0, w = BLK[blk]
                j = 2 if blk == 0 else b
                ri = rms_rstd(None, blk, 1.0 / D)
                r = rstd_buf[ri]
                sh_base = 0 if mf == 0 else 24
                for c in range(8):
                    ti = tmp_next()
                    t = tmpv(ti, F32)
                    S.op('dve', lambda e, t=t, c=c: e.tensor_tensor(t[:, :w], hT[:, c, t0:t0 + w], r[:, :w], ALU.mult),
                         reads=[U('h', c, blk), U('rstd', ri)], writes=[U('tmp', ti)])
                    S.op('act', lambda e, t=t, c=c, j=j: e.activation(xn[:, c, t0:t0 + w], t[:, :w], AF.Identity,
                                                                      bias=modT[:, l, sh_base + c, j:j + 1],
                                                                      scale=Gv[:, l, mf, j, c:c + 1]),
                         reads=[U('tmp', ti), U('modT'), U('Gv')], writes=[U('xn', c, blk)])

        def load_w(src_ap, slot):
            S.op('pool', lambda e: e.dma_start(out=wr[slot][:], in_=src_ap), writes=[U('w', slot)], dma=True)

        def project_T(wslot, dst_buf, dst_unit, blocks, rope):
            wv = wr[wslot].rearrange("p (k j) -> p k j", k=8)
            for blk in blocks:
                t0, w = BLK[blk]
                bank = work_next()
                def mm(e, bank=bank, t0=t0, w=w):
                    for kc in range(8):
                        r = e.matmul(psb[bank][:, :w], wv[:, kc, :], xn[:, kc, t0:t0 + w], start=(kc == 0), stop=(kc == 7))
                    return r
                S.op('pe', mm, reads=[U('w', wslot)] + [U('xn', c, blk) for c in range(8)], writes=[U('ps', bank)])
                if rope and blk > 0:
                    ti = tmp_next()
                    qpre = tmpv(ti, BF16)
                    S.op('act', lambda e, qpre=qpre, bank=bank, w=w: e.activation(qpre[:, :w], psb[bank][:, :w], AF.Copy),
                         reads=[U('ps', bank)], writes=[U('tmp', ti)])
                    bank2 = work_next()
                    S.op('pe', lambda e, qpre=qpre, bank2=bank2, w=w: e.matmul(psb[bank2][:, :w], rotm[:], qpre[:, :w], start=True, stop=True),
                         reads=[U('tmp', ti), U('rotm')], writes=[U('ps', bank2)])
                    l0 = t0 - LCTX
                    t1i = tmp_next()
                    t1 = tmpv(t1i, F32)
                    S.op('dve', lambda e, t1=t1, bank=bank, l0=l0, w=w: e.tensor_tensor(t1[:, :w], psb[bank][:, :w], cosT[:, l0:l0 + w], ALU.mult),
                         reads=[U('ps', bank), U('tab')], writes=[U('tmp', t1i)])
                    t2i = tmp_next()
                    t2 = tmpv(t2i, F32)
                    S.op('dve', lambda e, t2=t2, bank2=bank2, l0=l0, w=w: e.tensor_tensor(t2[:, :w], psb[bank2][:, :w], sinT[:, l0:l0 + w], ALU.mult),
                         reads=[U('ps', bank2), U('tab')], writes=[U('tmp', t2i)])
                    S.op(ROPE_ENG, lambda e, t1=t1, t2=t2, t0=t0, w=w: e.tensor_tensor(dst_buf[:, t0:t0 + w], t1[:, :w], t2[:, :w], ALU.add),
                         reads=[U('tmp', t1i), U('tmp', t2i)], writes=[U(dst_unit, blk)])
                else:
                    S.op('act', lambda e, bank=bank, t0=t0, w=w: e.activation(dst_buf[:, t0:t0 + w], psb[bank][:, :w], AF.Copy),
                         reads=[U('ps', bank)], writes=[U(dst_unit, blk)])

        def project_v(wslot):
            wv = wr[wslot].rearrange("p (k j) -> p k j", k=8)
            for q4 in range(5):
                tcs = list(range(q4 * 4, min(18, q4 * 4 + 4)))
                bank = work_next()
                def mm(e, bank=bank, tcs=tcs):
                    r = None
                    for i, tc in enumerate(tcs):
                        for kc in range(8):
                            r = e.matmul(psb[bank][:, i * 128:(i + 1) * 128], xn[:, kc, tc * 128:(tc + 1) * 128], wv[:, kc, :],
                                         start=(kc == 0), stop=(kc == 7), skip_group_check=True)
                    return r
                blks = sorted(set([0 if tc < 2 else 1 + (tc - 2) // 4 for tc in tcs]))
                S.op('pe', mm, reads=[U('w', wslot)] + [U('xn', c, bk) for c in range(8) for bk in blks], writes=[U('ps', bank)])
                n = len(tcs)
                S.op('act', lambda e, bank=bank, q4=q4, n=n: e.activation(
                    vtok[:, q4 * 4:q4 * 4 + n, :], psb[bank][:, 0:n * 128].rearrange("p (t e) -> p t e", t=n), AF.Copy),
                    reads=[U('ps', bank)], writes=[U('v', q4)])

        def shift_bounds(qbuf, kbuf, qunit, kunit, qblocks, nbias):
            for a, (buf, unit, blocks) in enumerate([(qbuf, qunit, qblocks), (kbuf, kunit, [0, 1, 2, 3, 4])]):
                for blk in blocks:
                    t0, w = BLK[blk]
                    ti = tmp_next()
                    sq = tmpv(ti, BF16)
                    S.op('act', lambda e, sq=sq, buf=buf, t0=t0, w=w: e.activation(sq[:, :w], buf[:, t0:t0 + w], AF.Square),
                         reads=[U(unit, blk)], writes=[U('tmp', ti)])
                    for m in range(2):
                        bank = work_next()
                        S.op('pe', lambda e, sq=sq, m=m, bank=bank, w=w: e.matmul(
                            psb[bank][:, :w], ones_bf[m * 64:(m + 1) * 64, :], sq[m * 64:(m + 1) * 64, :w], start=True, stop=True),
                            reads=[U('tmp', ti), U('ones')], writes=[U('ps', bank)])
                        S.op('dve', lambda e, a=a, m=m, blk=blk, bank=bank, w=w: e.tensor_reduce(
                            mx[:, a, m, blk:blk + 1], psb[bank][:, :w], AX.X, ALU.max),
                            reads=[U('ps', bank)], writes=[U('mx', a, m, blk)])
            m2 = small[:, 8:12]
            qb0 = min(qblocks)
            S.op('dve', lambda e: e.tensor_reduce(m2[:, 0:2], mx[:, 0, :, qb0:5], AX.X, ALU.max),
                 reads=[U('mx', 0, m, blk) for m in range(2) for blk in qblocks], writes=[U('m2')])
            S.op('dve', lambda e: e.tensor_reduce(m2[:, 2:4], mx[:, 1, :, 0:5], AX.X, ALU.max),
                 reads=[U('mx', 1, m, blk) for m in range(2) for blk in range(5)] + [U('m2')], writes=[U('m2')])
            c2 = small[:, 12:14]
            S.op('dve', lambda e: e.tensor_tensor(c2, m2[:, 0:2], m2[:, 2:4], ALU.mult), reads=[U('m2')], writes=[U('c2')])
            S.op('act', lambda e: e.activation(c2, c2, AF.Ln), reads=[U('c2')], writes=[U('c2')])
            S.op('act', lambda e: e.activation(c2, c2, AF.Exp, scale=0.5), reads=[U('c2')], writes=[U('c2')])
            S.op('dve', lambda e: e.tensor_scalar(nbias, c2, -0.125, None, ALU.mult), reads=[U('c2')], writes=[U('nbias')])

        def wo_accumulate(l, wslot, b, blocks):
            for oc in range(8):
                for blk in blocks:
                    t0, w = BLK[blk]
                    j = 2 if blk == 0 else b
                    bank = work_next()
                    S.op('pe', lambda e, bank=bank, oc=oc, t0=t0, w=w: e.matmul(
                        psb[bank][:, :w], wr[wslot][:, oc * 128:(oc + 1) * 128], oT[:, t0:t0 + w], start=True, stop=True),
                        reads=[U('w', wslot), U('oT', blk)], writes=[U('ps', bank)])
                    S.op('dve', lambda e, bank=bank, oc=oc, t0=t0, w=w, j=j: e.scalar_tensor_tensor(
                        hT[:, oc, t0:t0 + w], psb[bank][:, :w], modT[:, l, 16 + oc, j:j + 1], hT[:, oc, t0:t0 + w], ALU.mult, ALU.add),
                        reads=[U('ps', bank), U('modT'), U('h', oc, blk)], writes=[U('h', oc, blk)])

        def run_tiles(tiles, LA=3):
            for t in tiles:
                t['S'] = freeze(t['S'])
                t['acc'] = freeze(t['acc'])
            n = len(tiles)
            issued = 0
            for i in range(n + LA):
                if i < n:
                    t = tiles[i]
                    bank = work_next()
                    t['bank'] = bank
                    S.op('pe', lambda e, t=t, bank=bank: t['S'](e, psb[bank]), reads=t['s_reads'], writes=[U('ps', bank)])
                    pi = pt_next()
                    t['pt'] = pi
                    w = t['w']
                    S.op('act', lambda e, t=t, bank=bank, pi=pi, w=w: e.activation(PT[pi][:, :w], psb[bank][:, :w], AF.Exp,
                                                                                   bias=t['bias'], scale=0.125),
                         reads=[U('ps', bank), U('nbias')], writes=[U('pt', pi)])
                    if t.get('post') is not None:
                        t['post'](PT[pi], pi)
                k = i - LA
                if 0 <= k < n:
                    t = tiles[k]
                    pi = t['pt']
                    S.op('pe', lambda e, t=t, pi=pi: t['acc'](e, PT[pi]), reads=[U('pt', pi)] + t['a_reads'], writes=t['a_writes'])
                    if t.get('done') is not None:
                        t['done']()

        def diff_attention_layer(l, b, need_ctx):
            qblocks = [0, 1, 2, 3, 4] if need_ctx else [1, 2, 3, 4]
            for g in range(8):
                par = g % 2
                qT_, kT_ = big[par * 2], big[par * 2 + 1]
                qU, kU = ('B', par * 2), ('B', par * 2 + 1)
                sl = w_alloc(4)
                load_w(wqkv_r[l, 0, g], sl[0])
                load_w(wqkv_r[l, 1, g], sl[1])
                load_w(wqkv_r[l, 2, g], sl[2])
                load_w(wo_r[l, g], sl[3])
                project_T(sl[0], qT_, qU, qblocks, rope=True)
                chk('projq')
                project_T(sl[1], kT_, kU, [0, 1, 2, 3, 4], rope=True)
                project_v(sl[2])
                chk('proj')
                nbias = small[:, 16 + par * 2:18 + par * 2]
                nbU = U('nbias')
                shift_bounds(qT_, kT_, qU, kU, qblocks, nbias)
                chk('bounds')
                tiles = []
                for qb in qblocks:
                    t0, w = BLK[qb]
                    kts = [0, 1] if qb == 0 else list(range(18))
                    ep = {}
                    for m in range(2):
                        for ki, kt in enumerate(kts):
                            first_k = (ki == 0)
                            last_k = (ki == len(kts) - 1)
                            kblk = 0 if kt < 2 else 1 + (kt - 2) // 4
                            def Sfn(e, bank_ap, m=m, kt=kt, t0=t0, w=w):
                                return e.matmul(bank_ap[:, :w], kT_[m * 64:(m + 1) * 64, kt * 128:(kt + 1) * 128],
                                                qT_[m * 64:(m + 1) * 64, t0:t0 + w], start=True, stop=True)
                            def acc(e, pt, m=m, kt=kt, w=w, first_k=first_k, last_k=last_k):
                                e.matmul(psb[4 + m][:, :w], vtok[:, kt, :], pt[:, :w], start=first_k, stop=last_k)
                                return e.matmul(psb[6 + m][:, :w], ones_bf[:], pt[:, :w], start=first_k, stop=last_k)
                            tile = dict(S=Sfn, w=w, bias=nbias[:, m:m + 1], acc=acc,
                                        s_reads=[U(kU[0], kU[1], kblk), U(qU[0], qU[1], qb)],
                                        a_reads=[U('v', kt // 4), U('ones')], a_writes=[U('ps', 4 + m), U('ps', 6 + m)])
                            if last_k:
                                def done(m=m, qb=qb, t0=t0, w=w, ep=ep):
                                    ri = tmp_next()
                                    r = tmpv(ri, F32)
                                    S.op('dve', lambda e: e.reciprocal(r[:, :w], psb[6 + m][:, :w]),
                                         reads=[U('ps', 6 + m)], writes=[U('tmp', ri)])
                                    ti = tmp_next()
                                    t = tmpv(ti, F32)
                                    S.op('dve', lambda e: e.tensor_tensor(t[:, :w], psb[4 + m][:, :w], r[:, :w], ALU.mult),
                                         reads=[U('ps', 4 + m), U('tmp', ri)], writes=[U('tmp', ti)])
                                    ep[m] = ti
                                    if m == 1:
                                        t0i, t1i = ep[0], ep[1]
                                        ta, tb = tmpv(t0i, F32), tmpv(t1i, F32)
                                        S.op('dve', lambda e: e.scalar_tensor_tensor(ta[:, :w], tb[:, :w], negl, ta[:, :w], ALU.mult, ALU.add),
                                             reads=[U('tmp', t0i), U('tmp', t1i), U('negl')], writes=[U('tmp', t0i)])
                                        si = tmp_next()
                                        sq = tmpv(si, BF16)
                                        S.op('act', lambda e: e.activation(sq[:, :w], ta[:, :w], AF.Square),
                                             reads=[U('tmp', t0i)], writes=[U('tmp', si)])
                                        bank = work_next()
                                        S.op('pe', lambda e: e.matmul(psb[bank][:, :w], ones_bf[:], sq[:, :w], start=True, stop=True),
                                             reads=[U('tmp', si), U('ones')], writes=[U('ps', bank)])
                                        S.op('act', lambda e: e.activation(tb[:, :w], psb[bank][:, :w], AF.Ln, bias=eps_sb[:], scale=1.0 / 128),
                                             reads=[U('ps', bank), U('eps')], writes=[U('tmp', t1i)])
                                        S.op('act', lambda e: e.activation(tb[:, :w], tb[:, :w], AF.Exp, scale=-0.5),
                                             reads=[U('tmp', t1i)], writes=[U('tmp', t1i)])
                                        S.op('dve', lambda e: e.scalar_tensor_tensor(oT[:, t0:t0 + w], ta[:, :w], sublnS, tb[:, :w], ALU.mult, ALU.mult),
                                             reads=[U('tmp', t0i), U('tmp', t1i), U('sublnS')], writes=[U('oT', qb)])
                                tile['done'] = done
                            tiles.append(tile)
                run_tiles(tiles)
                chk('tiles')
                wo_accumulate(l, sl[3], b, qblocks)
                chk('head0')

        NAG = _na_groups()

        def na_tables(head, slot):
            for i in range(2):
                src = bass.AP(zscr_t, head * 15 * 127, [[1, 64], [127, 15], [1, 64]])
                S.op('sp', lambda e, i=i, src=src: e.dma_start(out=na_stg[i * 64:(i + 1) * 64, i + 3:i + 18, :], in_=src),
                     reads=[U('zscr')], writes=[U('nastg', i)], dma=True)
            full = na_tab[slot][0]
            intr = na_tab[slot][1]
            stg_rev = bass.AP(na_stg.tensor, na_stg.offset + 63, [list(na_stg.ap[0]), [64, 22], [-1, 64]])
            cm_rev = bass.AP(cmask.tensor, cmask.offset + 63, [list(cmask.ap[0]), [0, 22], [-1, 64]])
            S.op('dve', lambda e: e.tensor_tensor(full[:], stg_rev, cm_rev, ALU.mult),
                 reads=[U('nastg', 0), U('nastg', 1), U('cmask')], writes=[U('natab', slot, 0)])
            for i in range(2):
                S.op('pool', lambda e, i=i: e.tensor_copy(intr[i * 64:(i + 1) * 64, i + 7:i + 15, :], full[i * 64:(i + 1) * 64, i + 7:i + 15, :]),
                     reads=[U('natab', slot, 0)], writes=[U('natab', slot, 1, i)])

        def na_attention_layer(l, b):
            qblocks = [1, 2, 3, 4]
            for g in range(8):
                par = g % 2
                qT_, kT_ = big[par * 2], big[par * 2 + 1]
                qU, kU = ('B', par * 2), ('B', par * 2 + 1)
                sl = w_alloc(4)
                load_w(wqkv_r[l, 0, g], sl[0])
                load_w(wqkv_r[l, 1, g], sl[1])
                load_w(wqkv_r[l, 2, g], sl[2])
                load_w(wo_r[l, g], sl[3])
                project_T(sl[0], qT_, qU, qblocks, rope=False)
                project_T(sl[1], kT_, kU, [0, 1, 2, 3, 4], rope=False)
                project_v(sl[2])
                nbias = small[:, 16 + par * 2:18 + par * 2]
                shift_bounds(qT_, kT_, qU, kU, qblocks, nbias)
                tiles = []
                for hh in range(2):
                    head = 2 * g + hh
                    na_tables(head, hh)
                    for (r0, nr, kind, chs) in NAG:
                        w = nr * 64
                        t0 = LCTX + r0 * 64
                        qbs = sorted(set([1 + (r0 * 64) // 512, 1 + ((r0 + nr) * 64 - 1) // 512]))
                        klist = [('ctx', 0), ('ctx', 1)] + [('band', ch) for ch in chs]
                        for ki, (kk, ch) in enumerate(klist):
                            first_k = (ki == 0)
                            last_k = (ki == len(klist) - 1)
                            kt = ch if kk == 'ctx' else 2 + ch
                            kblk = 0 if kt < 2 else 1 + (kt - 2) // 4
                            def Sfn(e, bank_ap, hh=hh, kt=kt, t0=t0, w=w):
                                return e.matmul(bank_ap[:, :w], kT_[hh * 64:(hh + 1) * 64, kt * 128:(kt + 1) * 128],
                                                qT_[hh * 64:(hh + 1) * 64, t0:t0 + w], start=True, stop=True)
                            def acc(e, pt, hh=hh, kt=kt, w=w, first_k=first_k, last_k=last_k):
                                e.matmul(psb[4 + hh][:, :w], vtok[:, kt, :], pt[:, :w], start=first_k, stop=last_k)
                                return e.matmul(psb[6 + hh][:, :w], ones_bf[:], pt[:, :w], start=first_k, stop=last_k)
                            tile = dict(S=Sfn, w=w, bias=nbias[:, hh:hh + 1], acc=acc,
                                        s_reads=[U(kU[0], kU[1], kblk)] + [U(qU[0], qU[1], qb) for qb in qbs],
                                        a_reads=[U('v', kt // 4), U('ones')], a_writes=[U('ps', 4 + hh), U('ps', 6 + hh)])
                            if kk == 'band':
                                Dd = 2 * ch - r0
                                u0 = 7 - Dd + 3
                                assert 0 <= u0 and u0 + nr <= 22, (u0, nr)
                                tabv = na_tab[hh][0 if kind == 'full' else 1]
                                def post(pt, pi, tabv=tabv, u0=u0, nr=nr, w=w, hh=hh, kind=kind):
                                    rd = [U('natab', hh, 0)] if kind == 'full' else [U('natab', hh, 1, 0), U('natab', hh, 1, 1)]
                                    S.op('dve', lambda e: e.tensor_tensor(pt[:, :w], pt[:, :w],
                                                                          tabv[:, u0:u0 + nr, :].rearrange("p u c -> p (u c)"), ALU.mult),
                                         reads=[U('pt', pi)] + rd, writes=[U('pt', pi)])
                                tile['post'] = post
                            if last_k:
                                def done(hh=hh, t0=t0, w=w, qbs=qbs):
                                    ri = tmp_next()
                                    r = tmpv(ri, F32)
                                    hs = slice(hh * 64, (hh + 1) * 64)
                                    S.op('dve', lambda e: e.reciprocal(r[hs, :w], psb[6 + hh][hs, :w]),
                                         reads=[U('ps', 6 + hh)], writes=[U('tmp', ri)])
                                    S.op('dve', lambda e: e.tensor_tensor(oT[hs, t0:t0 + w], psb[4 + hh][hs, :w], r[hs, :w], ALU.mult),
                                         reads=[U('ps', 4 + hh), U('tmp', ri)] + [U('oT', qb) for qb in qbs],
                                         writes=[U('oT', qb) for qb in qbs])
                                tile['done'] = done
                            tiles.append(tile)
                run_tiles(tiles)
                wo_accumulate(l, sl[3], b, qblocks)

        def ffn_layer(l, b, blocks):
            norm_modulate(l, 1, b, blocks)
            JG = 4
            for j0 in range(0, NJ, JG):
                js = list(range(j0, min(NJ, j0 + JG)))
                wslots = {}
                for jj, j in enumerate(js):
                    sl = w_alloc(3)
                    wslots[j] = sl
                    S.op('pool', lambda e, sl=sl, j=j: e.dma_start(
                        out=wring[:, sl[0] * 1024:(sl[0] + 2) * 1024], in_=w1_r[l, j]),
                        writes=[U('w', sl[0]), U('w', sl[1])], dma=True)
                    load_w(w2_r[l, j], sl[2])
                    wg = wr[sl[0]].rearrange("p (k j) -> p k j", k=8)
                    wu = wr[sl[1]].rearrange("p (k j) -> p k j", k=8)
                    for blk in blocks:
                        t0, w = BLK[blk]
                        bg = work_next()
                        bu = work_next()
                        def mm(e, wg=wg, wu=wu, bg=bg, bu=bu, t0=t0, w=w):
                            for kc in range(8):
                                e.matmul(psb[bg][:, :w], wg[:, kc, :], xn[:, kc, t0:t0 + w], start=(kc == 0), stop=(kc == 7))
                            for kc in range(8):
                                r = e.matmul(psb[bu][:, :w], wu[:, kc, :], xn[:, kc, t0:t0 + w], start=(kc == 0), stop=(kc == 7))
                            return r
                        S.op('pe', mm, reads=[U('w', sl[0]), U('w', sl[1])] + [U('xn', c, blk) for c in range(8)],
                             writes=[U('ps', bg), U('ps', bu)])
                        ti = tmp_next()
                        sg = tmpv(ti, F32)
                        S.op('act', lambda e, sg=sg, bg=bg, w=w: e.activation(sg[:, :w], psb[bg][:, :w], AF.Silu),
                             reads=[U('ps', bg)], writes=[U('tmp', ti)])
                        S.op('dve', lambda e, sg=sg, bu=bu, jj=jj, t0=t0, w=w: e.tensor_tensor(big[jj][:, t0:t0 + w], sg[:, :w], psb[bu][:, :w], ALU.mult),
                             reads=[U('tmp', ti), U('ps', bu)], writes=[U('B', jj, blk)])
                for oc in range(8):
                    for blk in blocks:
                        t0, w = BLK[blk]
                        jv = 2 if blk == 0 else b
                        bank = work_next()
                        def mm2(e, bank=bank, oc=oc, t0=t0, w=w):
                            for jj, j in enumerate(js):
                                r = e.matmul(psb[bank][:, :w], wr[wslots[j][2]][:, oc * 128:(oc + 1) * 128], big[jj][:, t0:t0 + w],
                                             start=(jj == 0), stop=(jj == len(js) - 1))
                            return r
                        S.op('pe', mm2, reads=[U('w', wslots[j][2]) for j in js] + [U('B', jj, blk) for jj in range(len(js))],
                             writes=[U('ps', bank)])
                        S.op('dve', lambda e, bank=bank, oc=oc, t0=t0, w=w, jv=jv: e.scalar_tensor_tensor(
                            hT[:, oc, t0:t0 + w], psb[bank][:, :w], modT[:, l, 40 + oc, jv:jv + 1], hT[:, oc, t0:t0 + w], ALU.mult, ALU.add),
                            reads=[U('ps', bank), U('modT'), U('h', oc, blk)], writes=[U('h', oc, blk)])

        for b in range(nb):
            for c in range(8):
                for blk in range(5):
                    pass
                S.op('sp', lambda e, b=b, c=c: e.dma_start(out=hT[:, c, :], in_=xT[b, c * 128:(c + 1) * 128, :]),
                     writes=[U('h', c, blk) for blk in range(5)], dma=True)
            for l in layers:
              try:
                last = (l == 1)
                blocks = [1, 2, 3, 4] if last else [0, 1, 2, 3, 4]
                if stop == 'setup':
                    break
                norm_modulate(l, 0, b, [0, 1, 2, 3, 4])
                if stop == 'norm':
                    break
                if l == 0:
                    S.op('sp', lambda e: e.dma_start(out=tab[:].bitcast(F32).rearrange("p (a n) -> p a n", a=2), in_=ropet),
                         writes=[U('tab')] + [U('natab', s, 0) for s in range(2)] +
                                [U('natab', s, 1, i) for s in range(2) for i in range(2)] + [U('nastg', i) for i in range(2)], dma=True)
                    diff_attention_layer(l, b, need_ctx=not last)
                else:
                    S.op('pool', lambda e: e.memset(tab[:].bitcast(BF16), 0.0), reads=[U('tab')],
                         writes=[U('tab')] + [U('nastg', i) for i in range(2)] + [U('natab', s, 0) for s in range(2)] +
                                [U('natab', s, 1, i) for s in range(2) for i in range(2)])
                    na_attention_layer(l, b)
                if stop == 'attn':
                    break
                ffn_layer(l, b, blocks)
              except _Stop:
                break
            if final:
                for blk in [1, 2, 3, 4]:
                    t0, w = BLK[blk]
                    ri = rms_rstd(None, blk, 1.0 / D)
                    r = rstd_buf[ri]
                    for c in range(8):
                        ti = tmp_next()
                        t = tmpv(ti, F32)
                        S.op('dve', lambda e, t=t, c=c, t0=t0, w=w, r=r: e.scalar_tensor_tensor(
                            t[:, :w], hT[:, c, t0:t0 + w], nrm_sb[:, 4, c:c + 1], r[:, :w], ALU.mult, ALU.mult),
                            reads=[U('h', c, blk), U('rstd', ri), U('nrm')], writes=[U('tmp', ti)])
                        S.op('sp', lambda e, t=t, c=c, b=b, t0=t0, w=w: e.dma_start(
                            out=outT[b, c * 128:(c + 1) * 128, t0 - LCTX:t0 - LCTX + w], in_=t[:, :w]),
                            reads=[U('tmp', ti)], dma=True)
            else:
                for c in range(8):
                    S.op('sp', lambda e, b=b, c=c: e.dma_start(out=outT[b, c * 128:(c + 1) * 128, :], in_=hT[:, c, :]),
                         reads=[U('h', c, blk) for blk in range(5)], dma=True)
        S.emit()
    return nc


def _vecT(v):
    v = np.asarray(v, np.float32)
    return np.ascontiguousarray(np.moveaxis(v.reshape(v.shape[:-1] + (8, 128)), -1, 0))


def prepare_shared(inputs):
    f = lambda k: np.asarray(inputs[k], np.float32)
    sh = {}
    ada_w = f("ada_w")
    sh["ada_r"] = np.ascontiguousarray(ada_w.reshape(2, 8, 128, 6, 1024).transpose(0, 3, 1, 2, 4))
    sh["ada_bT"] = np.ascontiguousarray(f("ada_b").reshape(2, 48, 128).transpose(2, 0, 1))
    nrm = np.stack([f("norm_mix")[0], f("norm_mix")[1], f("norm_ffn")[0], f("norm_ffn")[1], f("norm_final")], 0)
    sh["nrm"] = np.ascontiguousarray(nrm.reshape(5, 8, 128).transpose(2, 0, 1))
    wqkv = np.stack([f("da_wqkv")[0], f("na_wqkv")[0]], 0)
    sh["wqkv_r"] = np.ascontiguousarray(wqkv.reshape(2, 8, 128, 3, 8, 128).transpose(0, 3, 4, 2, 1, 5)).reshape(2, 3, 8, 128, 1024)
    wo = np.stack([f("da_wo")[0], f("na_wo")[0]], 0)
    sh["wo_r"] = np.ascontiguousarray(wo.reshape(2, 8, 128, 1024))
    w1 = f("ffn_w_gate_up")
    sh["w1_r"] = np.ascontiguousarray(w1.reshape(2, 8, 128, 2, NJ, 128).transpose(0, 4, 2, 3, 1, 5)).reshape(2, NJ, 128, 2048)
    sh["w2_r"] = np.ascontiguousarray(f("ffn_w_down").reshape(2, NJ, 128, 1024))
    sh["lamv"] = np.ascontiguousarray(np.concatenate([f("da_lambda_q1")[0], f("da_lambda_k1")[0],
                                                      f("da_lambda_q2")[0], f("da_lambda_k2")[0]])[None, :])
    sh["subln"] = np.ascontiguousarray(f("da_subln")[0][:, None])
    rp = f("na_rpb")[0][:, ::-1, :]
    sh["rpb"] = np.ascontiguousarray(rp.reshape(120, 2, 31))
    sh["ropet"] = _rope_tables()
    sh["rotm"] = _rot_matrix()
    sh["cmask"] = _na_colmask_rev()
    return sh


def prepare_core(inputs, core, nb=NB):
    x = np.asarray(inputs["x"], np.float32)
    ctx = np.asarray(inputs["ctx"], np.float32)
    c = np.asarray(inputs["c"], np.float32)
    c_ctx = np.asarray(inputs["c_ctx"], np.float32)
    bs = slice(core * nb, (core + 1) * nb)
    xc = np.concatenate([ctx[bs], x[bs]], axis=1)
    m = {"xT": np.ascontiguousarray(xc.transpose(0, 2, 1))}
    cols = [c[core * nb + i] for i in range(nb)]
    while len(cols) < 2:
        cols.append(cols[-1])
    cols.append(c_ctx)
    cm = np.stack(cols, axis=-1)
    m["cT"] = np.ascontiguousarray(cm.reshape(8, 128, 3).transpose(1, 0, 2))
    return m


_PROG = {}


def kernel(**inputs):
    if 'full' not in _PROG:
        _PROG['full'] = build_program()
    nc = _PROG['full']
    sh = prepare_shared(inputs)
    in_maps = []
    for core in range(NCORES):
        m = dict(sh)
        m.update(prepare_core(inputs, core))
        in_maps.append(m)
    res = run_bass_kernel_spmd(nc, in_maps, core_ids=list(range(NCORES)))
    outs = [np.asarray(r["outT"]) for r in res.results]
    o = np.concatenate(outs, axis=0)
    return np.ascontiguousarray(o.transpose(0, 2, 1)).astype(np.float32)
```

```python
import contextlib
import numpy as np
import concourse.bass as bass
import concourse.mybir as mybir
from concourse.bass_utils import run_bass_kernel_spmd

F32 = mybir.dt.float32
BF16 = mybir.dt.bfloat16
U8 = mybir.dt.uint8
AF = mybir.ActivationFunctionType
ALU = mybir.AluOpType
AX = mybir.AxisListType

ENGS = ['pe', 'act', 'dve', 'pool', 'sp']

D = 1024
NTOK = 2304
LCTX = 256
NLAT = 2048
DFF = 2816
NJ = DFF // 128
EPS = 1e-6
NCORES = 8
NB = 2
import os
ROPE_ENG = os.environ.get('ROPE_ENG', 'dve')
FAST_RECIP = os.environ.get('FAST_RECIP', '0') == '1'


import types


def freeze(fn):
    if getattr(fn, '__closure__', None) is None:
        return fn
    cells = []
    for c in fn.__closure__:
        try:
            v = c.cell_contents
            if isinstance(v, types.FunctionType) and v is not fn:
                v = freeze(v)
            cells.append(types.CellType(v))
        except ValueError:
            cells.append(c)
    g = types.FunctionType(fn.__code__, fn.__globals__, fn.__name__, fn.__defaults__, tuple(cells))
    g.__kwdefaults__ = fn.__kwdefaults__
    return g


class _Op:
    __slots__ = ('id', 'eng', 'fn', 'chan', 'seq', 'waits', 'signal', 'clock', 'is_dma')


class Sched:
    def __init__(self, nc, dma_slots=None):
        self.nc = nc
        self.ops = []
        self.eng_ops = {e: [] for e in ENGS}
        self.units = {}
        self.eng_clock = {e: {} for e in ENGS}
        self.comp_seq = {e: 0 for e in ENGS}
        self.chan_ops = {}
        self.dma_slots = dma_slots or {'sp': 8, 'pool': 10, 'act': 4}
        self.dma_rr = {q: 0 for q in self.dma_slots}
        self.dma_cnt = {}
        self.dma_last = {}

    def op(self, eng, fn, reads=(), writes=(), dma=False):
        o = _Op()
        o.id = len(self.ops)
        o.eng = eng
        o.fn = freeze(fn)
        o.is_dma = dma
        o.signal = False
        deps = set()
        ps_reads = [u for u in reads if u[0] == 'ps']
        if ps_reads:
            reads = [u for u in reads if u[0] != 'ps']
            writes = list(writes) + [u for u in ps_reads if u not in writes]
        for u in reads:
            st = self.units.get(u)
            if st is not None and st[0] is not None:
                deps.add(st[0])
        for u in writes:
            st = self.units.get(u)
            if st is not None:
                if st[0] is not None:
                    deps.add(st[0])
                deps.update(st[1])
        if dma:
            q = eng
            slot = self.dma_rr[q]
            self.dma_rr[q] = (slot + 1) % self.dma_slots[q]
            chan = ('dma', q, slot)
            prev = self.dma_last.get(chan)
            if prev is not None:
                deps.add(prev)
            self.dma_cnt[chan] = self.dma_cnt.get(chan, 0) + 1
            o.chan = chan
            o.seq = self.dma_cnt[chan]
            self.dma_last[chan] = o.id
        else:
            self.comp_seq[eng] += 1
            o.chan = eng
            o.seq = self.comp_seq[eng]
        clk = self.eng_clock[eng]
        waits = {}
        for d in sorted(deps, reverse=True):
            dop = self.ops[d]
            if dop.chan == 'pe' and eng == 'pe' and not dma:
                continue
            if clk.get(dop.chan, 0) >= dop.seq:
                continue
            if waits.get(dop.chan, 0) < dop.seq:
                waits[dop.chan] = dop.seq
            for c, s in dop.clock.items():
                if clk.get(c, 0) < s:
                    clk[c] = s
        o.waits = list(waits.items())
        for c, s in o.waits:
            self.chan_ops[c][s - 1].signal = True
        o.clock = dict(clk)
        o.clock[o.chan] = o.seq
        self.chan_ops.setdefault(o.chan, []).append(o)
        self.ops.append(o)
        self.eng_ops[eng].append(o)
        for u in reads:
            st = self.units.get(u)
            if st is None:
                st = [None, []]
                self.units[u] = st
            st[1].append(o.id)
        for u in writes:
            self.units[u] = [o.id, []]
        return o

    def emit(self):
        nc = self.nc
        chans = list(self.chan_ops.keys())
        with contextlib.ExitStack() as es:
            sems = {}
            for c in chans:
                nm = 's_' + ('_'.join(str(x) for x in c) if isinstance(c, tuple) else c)
                sems[c] = es.enter_context(nc.semaphore(nm))
            sigcount = {}
            for c, lst in self.chan_ops.items():
                if isinstance(c, tuple):
                    continue
                n = 0
                arr = []
                for o in lst:
                    if o.signal:
                        n += 1
                    arr.append(n)
                sigcount[c] = arr

            def wval(c, s):
                if isinstance(c, tuple):
                    return 16 * s
                return sigcount[c][s - 1]

            block = es.enter_context(nc.Block())
            engmap = {'pe': block.tensor, 'act': block.scalar, 'dve': block.vector,
                      'pool': block.gpsimd, 'sp': block.sync}
            for e in ENGS:
                lst = self.eng_ops[e]
                if not lst:
                    continue
                fin = [(c, self.dma_cnt[c]) for c in chans if isinstance(c, tuple) and c[1] == e]

                def body(engine, lst=lst, fin=fin):
                    for o in lst:
                        for c, s in o.waits:
                            engine.wait_ge(sems[c], wval(c, s))
                        ins = o.fn(engine)
                        if o.is_dma:
                            ins.then_inc(sems[o.chan], 16)
                        elif o.signal:
                            ins.then_inc(sems[o.chan], 1)
                    for c, n in fin:
                        engine.wait_ge(sems[c], 16 * n)
                engmap[e](body)


def _rope_tables():
    t = np.arange(NLAT)
    row = (t // 64).astype(np.float32)
    col = (t % 64).astype(np.float32)
    half = 32
    freqs = (1.0 / (np.float32(10000.0) ** (np.arange(0, half, 2, dtype=np.float32) / np.float32(half)))).astype(np.float32)
    ar = row[:, None] * freqs
    ac = col[:, None] * freqs
    ang = np.concatenate([ar, ar, ac, ac], axis=-1).astype(np.float32)
    cos = np.cos(ang).astype(np.float32).T
    sin = np.sin(ang).astype(np.float32).T
    cs = np.stack([np.concatenate([cos, cos], 0), np.concatenate([sin, sin], 0)], axis=1)
    return np.ascontiguousarray(cs)


def _rot_matrix():
    R = np.zeros((128, 128), np.float32)
    for m in range(128):
        d = m % 32
        base = m - d
        if d < 16:
            R[base + d + 16, m] = -1.0
        else:
            R[base + d - 16, m] = 1.0
    return R


def _na_colmask_rev():
    qc = np.arange(64)
    cs = np.clip(qc - 8, 0, 48)
    m = np.zeros((64, 64), np.float32)
    for c in range(64):
        m[cs[c]:cs[c] + 16, c] = 1.0
    mrev = m[:, ::-1]
    return np.ascontiguousarray(np.concatenate([mrev, mrev], 0))


def _na_groups():
    groups = []
    def chunks(lo_row, hi_row):
        return list(range(lo_row // 2, hi_row // 2 + 1))
    groups.append((0, 4, 'full', chunks(0, 7)))
    groups.append((4, 4, 'int', chunks(0, 10)))
    groups.append((8, 8, 'int', chunks(4, 18)))
    groups.append((16, 8, 'int', chunks(12, 26)))
    groups.append((24, 5, 'int', chunks(20, 31)))
    groups.append((29, 3, 'full', chunks(24, 31)))
    return groups


class _Stop(Exception):
    pass


def build_program(nb=NB, layers=(0, 1), first=True, final=True, dbg=False, stop=None):
    def chk(stage):
        if stop == stage:
            raise _Stop()
    nc = bass.Bass("TRN2", target_bir_lowering=False)

    def din(name, shape, dt=F32):
        return nc.dram_tensor(name, list(shape), dt, kind="ExternalInput").ap()

    xT = din("xT", [nb, D, NTOK])
    cT = din("cT", [128, 8, 3])
    ada_r = din("ada_r", [2, 6, 8, 128, 1024])
    ada_bT = din("ada_bT", [128, 2, 48])
    nrm = din("nrm", [128, 5, 8])
    wqkv_r = din("wqkv_r", [2, 3, 8, 128, 1024])
    wo_r = din("wo_r", [2, 8, 128, 1024])
    w1_r = din("w1_r", [2, NJ, 128, 2048])
    w2_r = din("w2_r", [2, NJ, 128, 1024])
    lamv = din("lamv", [1, 256])
    subln = din("subln", [128, 1])
    rpb = din("rpb", [120, 2, 31])
    ropet = din("ropet", [128, 2, NLAT])
    rotm_d = din("rotm", [128, 128])
    cmask_d = din("cmask", [128, 64])
    outT = nc.dram_tensor("outT", [nb, D, NLAT if final else NTOK], F32, kind="ExternalOutput").ap()
    zscr_t = nc.dram_tensor("zscr", [240, 127], BF16)
    zscr = zscr_t.ap()

    S = Sched(nc)
    es = contextlib.ExitStack()
    with es:
        ARENA = 212000
        arena = nc.alloc_sbuf_tensor("arena", [128, ARENA], U8)
        off = [0]

        def carve(nbytes, dt, pattern=None, **kw):
            a = off[0]
            off[0] += (nbytes + 31) // 32 * 32
            assert off[0] <= ARENA, off[0]
            v = arena[:, a:a + nbytes].bitcast(dt)
            if pattern:
                v = v.rearrange(pattern, **kw)
            return v

        hT = carve(8 * NTOK * 4, F32, "p (c t) -> p c t", c=8)
        xn = carve(8 * NTOK * 2, BF16, "p (c t) -> p c t", c=8)
        big = [carve(NTOK * 2, BF16) for _ in range(4)]
        vtok = carve(18 * 128 * 2, BF16, "p (t e) -> p t e", t=18)
        oT = carve(NTOK * 2, BF16)
        NPT = 6
        PT = [carve(512 * 2, BF16) for _ in range(NPT)]
        NW = 12
        wring = carve(NW * 2048, BF16)
        wr = [wring[:, i * 1024:(i + 1) * 1024] for i in range(NW)]
        rstd_buf = [carve(2048, F32) for _ in range(2)]
        dacc = [carve(2048, F32) for _ in range(2)]
        lamtmp = carve(160 * 4, F32)
        tab = carve(16384, U8)
        NTMP = 6
        tmp = [carve(2048, U8) for _ in range(NTMP)]
        ones_bf = carve(128 * 2, BF16)
        rotm = carve(128 * 2, BF16)
        ones_f = carve(128 * 4, F32)
        modT = carve(2 * 48 * 3 * 4, F32, "p (l c j) -> p l c j", l=2, c=48)
        Gv = carve(2 * 2 * 3 * 8 * 4, F32, "p (l m j c) -> p l m j c", l=2, m=2, j=3)
        nrm_sb = carve(5 * 8 * 4, F32, "p (n c) -> p n c", n=5)
        adab_sb = carve(2 * 48 * 4, F32, "p (l c) -> p l c", l=2)
        c_sb = carve(8 * 3 * 4, F32, "p (k j) -> p k j", k=8)
        sc_bf = carve(8 * 3 * 2, BF16, "p (k j) -> p k j", k=8)
        small = carve(64 * 4, F32)
        lam_sb = carve(256 * 4, F32)
        subln_sb = carve(4, F32)
        eps_sb = carve(4, F32)
        mx = carve(2 * 2 * 5 * 4, F32, "p (a m b) -> p a m b", a=2, m=2)
        cmask = carve(64 * 2, BF16)
        ps_t = nc.alloc_psum_tensor("ps", [128, 8, 512], F32)
        psb = [ps_t[:, k, :] for k in range(8)]

        cosT = tab[:, 0:8192].bitcast(F32)
        sinT = tab[:, 8192:16384].bitcast(F32)
        na_stg = tab[:, 0:2816].bitcast(BF16).rearrange("p (u c) -> p u c", u=22)
        na_tab = [[tab[:, 2816 + (s * 2 + v) * 2816: 2816 + (s * 2 + v + 1) * 2816].bitcast(BF16)
                   .rearrange("p (u c) -> p u c", u=22) for v in range(2)] for s in range(2)]

        def tmpv(i, dt, w=512):
            nbytes = w * (4 if dt == F32 else 2)
            return tmp[i][:, 0:nbytes].bitcast(dt)

        st = {'tmp': 0, 'pt': 0, 'w': 0, 'wk': 0, 'rs': 0, 'pw': 0}

        def tmp_next():
            i = st['tmp']
            st['tmp'] = (i + 1) % NTMP
            return i

        def pt_next():
            i = st['pt']
            st['pt'] = (i + 1) % NPT
            return i

        def w_alloc(n):
            p = st['w']
            if p + n > NW:
                p = 0
            st['w'] = p + n
            return list(range(p, p + n))

        WORK = [0, 1, 2, 3]

        def work_next():
            i = st['wk']
            st['wk'] = (i + 1) % len(WORK)
            return WORK[i]

        def U(*a):
            return a

        BLK = [(0, 256)] + [(256 + 512 * i, 512) for i in range(4)]

        S.op('pool', lambda e: e.memset(ones_bf[:], 1.0), writes=[U('ones')])
        S.op('pool', lambda e: e.memset(ones_f[:], 1.0), writes=[U('onesf')])
        S.op('pool', lambda e: e.memset(eps_sb[:], EPS), writes=[U('eps')])
        S.op('pool', lambda e: e.dma_start(out=rotm[:], in_=rotm_d), writes=[U('rotm')], dma=True)
        S.op('pool', lambda e: e.dma_start(out=cmask[:], in_=cmask_d), writes=[U('cmask')], dma=True)
        S.op('sp', lambda e: e.dma_start(out=nrm_sb[:], in_=nrm), writes=[U('nrm')], dma=True)
        S.op('sp', lambda e: e.dma_start(out=adab_sb[:], in_=ada_bT), writes=[U('adab')], dma=True)
        S.op('sp', lambda e: e.dma_start(out=c_sb[:], in_=cT), writes=[U('c')], dma=True)
        S.op('sp', lambda e: e.dma_start(out=lam_sb[0:1, :], in_=lamv), writes=[U('lamrow')], dma=True)
        S.op('sp', lambda e: e.dma_start(out=subln_sb[:], in_=subln), writes=[U('subln')], dma=True)

        S.op('act', lambda e: e.activation(sc_bf[:], c_sb[:], AF.Silu), reads=[U('c')], writes=[U('sc')])

        if first:
            for l in layers:
                for grp in range(6):
                    slots = []
                    for kc in range(8):
                        s = w_alloc(1)[0]
                        slots.append(s)
                        S.op('pool', lambda e, s=s, l=l, grp=grp, kc=kc: e.dma_start(out=wr[s][:], in_=ada_r[l, grp, kc]),
                             writes=[U('w', s)], dma=True)
                        for o in range(8):
                            S.op('pe', lambda e, s=s, o=o, kc=kc: e.matmul(psb[o][:, 0:3], wr[s][:, o * 128:(o + 1) * 128],
                                                                            sc_bf[:, kc, :], start=(kc == 0), stop=(kc == 7)),
                                 reads=[U('w', s), U('sc')], writes=[U('ps', o)])
                    for o in range(8):
                        ch = grp * 8 + o
                        S.op('dve', lambda e, o=o, l=l, ch=ch: e.tensor_scalar(modT[:, l, ch, :], psb[o][:, 0:3],
                                                                                   adab_sb[:, l, ch:ch + 1], None, ALU.add),
                             reads=[U('ps', o), U('adab')], writes=[U('modT')])
            for l in layers:
                for mf in range(2):
                    for j in range(3):
                        sc_base = 8 if mf == 0 else 32
                        nidx = l if mf == 0 else 2 + l
                        S.op('dve', lambda e, l=l, mf=mf, j=j, sc_base=sc_base, nidx=nidx: e.scalar_tensor_tensor(
                            Gv[:, l, mf, j, :], modT[:, l, sc_base:sc_base + 8, j], 1.0, nrm_sb[:, nidx, :], ALU.add, ALU.mult),
                            reads=[U('modT'), U('nrm')], writes=[U('Gv')])

        LAM_INIT = 0.8 - 0.6 * float(np.exp(-0.3 * 0))
        negl = small[:, 0:1]
        sublnS = small[:, 1:2]
        if 0 in layers:
            lr = lam_sb[0:1, :]
            prod = lamtmp
            S.op('dve', lambda e: e.tensor_tensor(prod[0:1, 0:64], lr[:, 0:64], lr[:, 64:128], ALU.mult),
                 reads=[U('lamrow')], writes=[U('lamtmp')])
            S.op('dve', lambda e: e.tensor_tensor(prod[0:1, 64:128], lr[:, 128:192], lr[:, 192:256], ALU.mult),
                 reads=[U('lamrow'), U('lamtmp')], writes=[U('lamtmp')])
            S.op('dve', lambda e: e.tensor_reduce(prod[0:1, 128:130], prod[0:1, 0:128].rearrange("p (a b) -> p a b", a=2),
                                                  AX.X, ALU.add), reads=[U('lamtmp')], writes=[U('lamtmp')])
            S.op('act', lambda e: e.activation(prod[0:1, 130:132], prod[0:1, 128:130], AF.Exp),
                 reads=[U('lamtmp')], writes=[U('lamtmp')])
            S.op('dve', lambda e: e.tensor_tensor(prod[0:1, 132:133], prod[0:1, 131:132], prod[0:1, 130:131], ALU.subtract),
                 reads=[U('lamtmp')], writes=[U('lamtmp')])
            S.op('dve', lambda e: e.tensor_scalar(prod[0:1, 133:134], prod[0:1, 132:133], -LAM_INIT, None, ALU.add),
                 reads=[U('lamtmp')], writes=[U('lamtmp')])
            S.op('pe', lambda e: e.matmul(psb[0][:, 0:1], ones_f[0:1, :], prod[0:1, 133:134], start=True, stop=True),
                 reads=[U('lamtmp'), U('onesf')], writes=[U('ps', 0)])
            S.op('dve', lambda e: e.tensor_copy(negl, psb[0][:, 0:1]), reads=[U('ps', 0)], writes=[U('negl')])
            S.op('dve', lambda e: e.tensor_scalar(sublnS, subln_sb[:], 1.0 - LAM_INIT, None, ALU.mult),
                 reads=[U('subln')], writes=[U('sublnS')])

        if 1 in layers:
            zt = tmpv(1, BF16, 512)
            rp = tmpv(2, F32, 512)
            zv = zt[0:120, 0:254].rearrange("p (a s) -> p a s", a=2)
            S.op('sp', lambda e: e.dma_start(out=rp[0:120, 0:62].rearrange("p (a s) -> p a s", a=2), in_=rpb),
                 writes=[U('tmp', 2)], dma=True)
            S.op('pool', lambda e: e.memset(zt[0:120, 0:254], 0.0), writes=[U('tmp', 1)])
            S.op('act', lambda e: e.activation(zv[:, :, 48:79], rp[0:120, 0:62].rearrange("p (a s) -> p a s", a=2), AF.Exp),
                 reads=[U('tmp', 2), U('tmp', 1)], writes=[U('tmp', 1)])
            S.op('sp', lambda e: e.dma_start(out=zscr.rearrange("(p a) s -> p a s", a=2), in_=zv),
                 reads=[U('tmp', 1)], writes=[U('zscr')], dma=True)

        def rms_rstd(src_blocks, blk, inv_n, nchunk_parts=128):
            t0, w = BLK[blk]
            bank = work_next()
            for c in range(8):
                ti = tmp_next()
                sq = tmpv(ti, BF16)
                S.op('act', lambda e, sq=sq, c=c: e.activation(sq[:, :w], hT[:, c, t0:t0 + w], AF.Square),
                     reads=[U('h', c, blk)], writes=[U('tmp', ti)])
                S.op('pe', lambda e, sq=sq, c=c: e.matmul(psb[bank][:, :w], ones_bf[:], sq[:, :w], start=(c == 0), stop=(c == 7)),
                     reads=[U('tmp', ti), U('ones')], writes=[U('ps', bank)])
            ri = st['rs']
            st['rs'] = 1 - ri
            r = rstd_buf[ri]
            S.op('act', lambda e: e.activation(r[:, :w], psb[bank][:, :w], AF.Ln, bias=eps_sb[:], scale=inv_n),
                 reads=[U('ps', bank), U('eps')], writes=[U('rstd', ri)])
            S.op('act', lambda e: e.activation(r[:, :w], r[:, :w], AF.Exp, scale=-0.5),
                 reads=[U('rstd', ri)], writes=[U('rstd', ri)])
            return ri

        def norm_modulate(l, mf, b, blocks):
            for blk in blocks:
                t0, w = BLK[blk]
                j = 2 if blk == 0 else b
                ri = rms_rstd(None, blk, 1.0 / D)
                r = rstd_buf[ri]
                sh_base = 0 if mf == 0 else 24
                for c in range(8):
                    ti = tmp_next()
                    t = tmpv(ti, F32)
                    S.op('dve', lambda e, t=t, c=c: e.tensor_tensor(t[:, :w], hT[:, c, t0:t0 + w], r[:, :w], ALU.mult),
                         reads=[U('h', c, blk), U('rstd', ri)], writes=[U('tmp', ti)])
                    S.op('act', lambda e, t=t, c=c, j=j: e.activation(xn[:, c, t0:t0 + w], t[:, :w], AF.Identity,
                                                                      bias=modT[:, l, sh_base + c, j:j + 1],
                                                                      scale=Gv[:, l, mf, j, c:c + 1]),
                         reads=[U('tmp', ti), U('modT'), U('Gv')], writes=[U('xn', c, blk)])

        def load_w(src_ap, slot):
            S.op('pool', lambda e: e.dma_start(out=wr[slot][:], in_=src_ap), writes=[U('w', slot)], dma=True)

        def project_T(wslot, dst_buf, dst_unit, blocks, rope):
            wv = wr[wslot].rearrange("p (k j) -> p k j", k=8)
            for blk in blocks:
                t0, w = BLK[blk]
                bank = work_next()
                def mm(e, bank=bank, t0=t0, w=w):
                    for kc in range(8):
                        r = e.matmul(psb[bank][:, :w], wv[:, kc, :], xn[:, kc, t0:t0 + w], start=(kc == 0), stop=(kc == 7))
                    return r
                S.op('pe', mm, reads=[U('w', wslot)] + [U('xn', c, blk) for c in range(8)], writes=[U('ps', bank)])
                if rope and blk > 0:
                    ti = tmp_next()
                    qpre = tmpv(ti, BF16)
                    S.op('act', lambda e, qpre=qpre, bank=bank, w=w: e.activation(qpre[:, :w], psb[bank][:, :w], AF.Copy),
                         reads=[U('ps', bank)], writes=[U('tmp', ti)])
                    bank2 = work_next()
                    S.op('pe', lambda e, qpre=qpre, bank2=bank2, w=w: e.matmul(psb[bank2][:, :w], rotm[:], qpre[:, :w], start=True, stop=True),
                         reads=[U('tmp', ti), U('rotm')], writes=[U('ps', bank2)])
                    l0 = t0 - LCTX
                    t1i = tmp_next()
                    t1 = tmpv(t1i, F32)
                    S.op('dve', lambda e, t1=t1, bank=bank, l0=l0, w=w: e.tensor_tensor(t1[:, :w], psb[bank][:, :w], cosT[:, l0:l0 + w], ALU.mult),
                         reads=[U('ps', bank), U('tab')], writes=[U('tmp', t1i)])
                    t2i = tmp_next()
                    t2 = tmpv(t2i, F32)
                    S.op('dve', lambda e, t2=t2, bank2=bank2, l0=l0, w=w: e.tensor_tensor(t2[:, :w], psb[bank2][:, :w], sinT[:, l0:l0 + w], ALU.mult),
                         reads=[U('ps', bank2), U('tab')], writes=[U('tmp', t2i)])
                    S.op(ROPE_ENG, lambda e, t1=t1, t2=t2, t0=t0, w=w: e.tensor_tensor(dst_buf[:, t0:t0 + w], t1[:, :w], t2[:, :w], ALU.add),
                         reads=[U('tmp', t1i), U('tmp', t2i)], writes=[U(dst_unit, blk)])
                else:
                    S.op('act', lambda e, bank=bank, t0=t0, w=w: e.activation(dst_buf[:, t0:t0 + w], psb[bank][:, :w], AF.Copy),
                         reads=[U('ps', bank)], writes=[U(dst_unit, blk)])

        def project_v(wslot):
            wv = wr[wslot].rearrange("p (k j) -> p k j", k=8)
            for q4 in range(5):
                tcs = list(range(q4 * 4, min(18, q4 * 4 + 4)))
                bank = work_next()
                def mm(e, bank=bank, tcs=tcs):
                    r = None
                    for i, tc in enumerate(tcs):
                        for kc in range(8):
                            r = e.matmul(psb[bank][:, i * 128:(i + 1) * 128], xn[:, kc, tc * 128:(tc + 1) * 128], wv[:, kc, :],
                                         start=(kc == 0), stop=(kc == 7), skip_group_check=True)
                    return r
                blks = sorted(set([0 if tc < 2 else 1 + (tc - 2) // 4 for tc in tcs]))
                S.op('pe', mm, reads=[U('w', wslot)] + [U('xn', c, bk) for c in range(8) for bk in blks], writes=[U('ps', bank)])
                n = len(tcs)
                S.op('act', lambda e, bank=bank, q4=q4, n=n: e.activation(
                    vtok[:, q4 * 4:q4 * 4 + n, :], psb[bank][:, 0:n * 128].rearrange("p (t e) -> p t e", t=n), AF.Copy),
                    reads=[U('ps', bank)], writes=[U('v', q4)])

        def shift_bounds(qbuf, kbuf, qunit, kunit, qblocks, nbias):
            for a, (buf, unit, blocks) in enumerate([(qbuf, qunit, qblocks), (kbuf, kunit, [0, 1, 2, 3, 4])]):
                for blk in blocks:
                    t0, w = BLK[blk]
                    ti = tmp_next()
                    sq = tmpv(ti, BF16)
                    S.op('act', lambda e, sq=sq, buf=buf, t0=t0, w=w: e.activation(sq[:, :w], buf[:, t0:t0 + w], AF.Square),
                         reads=[U(unit, blk)], writes=[U('tmp', ti)])
                    for m in range(2):
                        bank = work_next()
                        S.op('pe', lambda e, sq=sq, m=m, bank=bank, w=w: e.matmul(
                            psb[bank][:, :w], ones_bf[m * 64:(m + 1) * 64, :], sq[m * 64:(m + 1) * 64, :w], start=True, stop=True),
                            reads=[U('tmp', ti), U('ones')], writes=[U('ps', bank)])
                        S.op('dve', lambda e, a=a, m=m, blk=blk, bank=bank, w=w: e.tensor_reduce(
                            mx[:, a, m, blk:blk + 1], psb[bank][:, :w], AX.X, ALU.max),
                            reads=[U('ps', bank)], writes=[U('mx', a, m, blk)])
            m2 = small[:, 8:12]
            qb0 = min(qblocks)
            S.op('dve', lambda e: e.tensor_reduce(m2[:, 0:2], mx[:, 0, :, qb0:5], AX.X, ALU.max),
                 reads=[U('mx', 0, m, blk) for m in range(2) for blk in qblocks], writes=[U('m2')])
            S.op('dve', lambda e: e.tensor_reduce(m2[:, 2:4], mx[:, 1, :, 0:5], AX.X, ALU.max),
                 reads=[U('mx', 1, m, blk) for m in range(2) for blk in range(5)] + [U('m2')], writes=[U('m2')])
            c2 = small[:, 12:14]
            S.op('dve', lambda e: e.tensor_tensor(c2, m2[:, 0:2], m2[:, 2:4], ALU.mult), reads=[U('m2')], writes=[U('c2')])
            S.op('act', lambda e: e.activation(c2, c2, AF.Ln), reads=[U('c2')], writes=[U('c2')])
            S.op('act', lambda e: e.activation(c2, c2, AF.Exp, scale=0.5), reads=[U('c2')], writes=[U('c2')])
            S.op('dve', lambda e: e.tensor_scalar(nbias, c2, -0.125, None, ALU.mult), reads=[U('c2')], writes=[U('nbias')])

        def wo_accumulate(l, wslot, b, blocks):
            for oc in range(8):
                for blk in blocks:
                    t0, w = BLK[blk]
                    j = 2 if blk == 0 else b
                    bank = work_next()
                    S.op('pe', lambda e, bank=bank, oc=oc, t0=t0, w=w: e.matmul(
                        psb[bank][:, :w], wr[wslot][:, oc * 128:(oc + 1) * 128], oT[:, t0:t0 + w], start=True, stop=True),
                        reads=[U('w', wslot), U('oT', blk)], writes=[U('ps', bank)])
                    S.op('dve', lambda e, bank=bank, oc=oc, t0=t0, w=w, j=j: e.scalar_tensor_tensor(
                        hT[:, oc, t0:t0 + w], psb[bank][:, :w], modT[:, l, 16 + oc, j:j + 1], hT[:, oc, t0:t0 + w], ALU.mult, ALU.add),
                        reads=[U('ps', bank), U('modT'), U('h', oc, blk)], writes=[U('h', oc, blk)])

        PAIRS = [(0, 1), (2, 3), (6, 7)]
        DACC_ENG = ['dve', 'pool']

        def pair_next():
            i = st['pw']
            st['pw'] = (i + 1) % len(PAIRS)
            return PAIRS[i]

        def run_pairs(pairs, LA=2):
            for p in pairs:
                p['S'] = freeze(p['S'])
                p['acc'] = freeze(p['acc'])
            n = len(pairs)
            for i in range(n + LA):
                if i < n:
                    p = pairs[i]
                    ba, bb = pair_next()
                    w = p['w']
                    S.op('pe', lambda e, p=p, ba=ba, bb=bb: p['S'](e, psb[ba], psb[bb]), reads=p['s_reads'],
                         writes=[U('ps', ba), U('ps', bb)])
                    pis = []
                    for hf, bank in enumerate((ba, bb)):
                        pi = pt_next()
                        pis.append(pi)
                        S.op('act', lambda e, p=p, bank=bank, pi=pi, w=w, hf=hf: e.activation(
                            PT[pi][:, :w], psb[bank][:, :w], AF.Exp, bias=p['bias'][hf], scale=0.125),
                            reads=[U('ps', bank), U('nbias')], writes=[U('pt', pi)])
                        if p.get('post') is not None:
                            p['post'](hf, PT[pi], pi)
                        eng = DACC_ENG[hf]
                        if p['first']:
                            S.op(eng, lambda e, pi=pi, hf=hf, w=w: e.tensor_copy(dacc[hf][:, :w], PT[pi][:, :w]),
                                 reads=[U('pt', pi)], writes=[U('dacc', hf)])
                        else:
                            S.op(eng, lambda e, pi=pi, hf=hf, w=w: e.tensor_tensor(dacc[hf][:, :w], dacc[hf][:, :w], PT[pi][:, :w], ALU.add),
                                 reads=[U('pt', pi), U('dacc', hf)], writes=[U('dacc', hf)])
                    p['pts'] = pis
                    if p.get('pre_done') is not None:
                        p['pre_done']()
                k = i - LA
                if 0 <= k < n:
                    p = pairs[k]
                    pa, pb = p['pts']
                    S.op('pe', lambda e, p=p, pa=pa, pb=pb: p['acc'](e, PT[pa], PT[pb]),
                         reads=[U('pt', pa), U('pt', pb)] + p['a_reads'], writes=[U('ps', 4), U('ps', 5)])
                    if p.get('done') is not None:
                        p['done']()

        def den_recip(hf, w, rows=slice(0, 128)):
            eng = DACC_ENG[hf]
            hi_i = tmp_next()
            hi = tmpv(hi_i, BF16)
            S.op(eng, lambda e: e.tensor_copy(hi[:, :w], dacc[hf][:, :w]), reads=[U('dacc', hf)], writes=[U('tmp', hi_i)])
            lo_i = tmp_next()
            lo = tmpv(lo_i, BF16)
            S.op(eng, lambda e: e.tensor_tensor(lo[:, :w], dacc[hf][:, :w], hi[:, :w], ALU.subtract),
                 reads=[U('dacc', hf), U('tmp', hi_i)], writes=[U('tmp', lo_i)])
            bank = work_next()
            def mm(e):
                e.matmul(psb[bank][:, :w], ones_bf[:], hi[:, :w], start=True, stop=False)
                return e.matmul(psb[bank][:, :w], ones_bf[:], lo[:, :w], start=False, stop=True)
            S.op('pe', mm, reads=[U('tmp', hi_i), U('tmp', lo_i), U('ones')], writes=[U('ps', bank)])
            ri = tmp_next()
            r = tmpv(ri, F32)
            S.op('dve', lambda e: (e.reciprocal_approx_fast(r[rows, :w], psb[bank][rows, :w]) if FAST_RECIP else e.reciprocal(r[rows, :w], psb[bank][rows, :w])), reads=[U('ps', bank)], writes=[U('tmp', ri)])
            return ri

        def diff_attention_layer(l, b, need_ctx):
            qblocks = [0, 1, 2, 3, 4] if need_ctx else [1, 2, 3, 4]
            for g in range(8):
                par = g % 2
                qT_, kT_ = big[par * 2], big[par * 2 + 1]
                qU, kU = ('B', par * 2), ('B', par * 2 + 1)
                sl = w_alloc(4)
                load_w(wqkv_r[l, 0, g], sl[0])
                load_w(wqkv_r[l, 1, g], sl[1])
                load_w(wqkv_r[l, 2, g], sl[2])
                load_w(wo_r[l, g], sl[3])
                project_T(sl[0], qT_, qU, qblocks, rope=True)
                chk('projq')
                project_T(sl[1], kT_, kU, [0, 1, 2, 3, 4], rope=True)
                project_v(sl[2])
                chk('proj')
                nbias = small[:, 16 + par * 2:18 + par * 2]
                shift_bounds(qT_, kT_, qU, kU, qblocks, nbias)
                chk('bounds')
                pairs = []
                for qb in qblocks:
                    t0, w = BLK[qb]
                    kts = [0, 1] if qb == 0 else list(range(18))
                    for ki, kt in enumerate(kts):
                        first_k = (ki == 0)
                        last_k = (ki == len(kts) - 1)
                        kblk = 0 if kt < 2 else 1 + (kt - 2) // 4
                        def Sfn(e, ba, bb, kt=kt, t0=t0, w=w):
                            e.matmul(ba[:, :w], kT_[0:64, kt * 128:(kt + 1) * 128], qT_[0:64, t0:t0 + w], start=True, stop=True)
                            return e.matmul(bb[:, :w], kT_[64:128, kt * 128:(kt + 1) * 128], qT_[64:128, t0:t0 + w], start=True, stop=True)
                        def acc(e, pa, pb, kt=kt, w=w, first_k=first_k, last_k=last_k):
                            e.matmul(psb[4][:, :w], vtok[:, kt, :], pa[:, :w], start=first_k, stop=last_k)
                            return e.matmul(psb[5][:, :w], vtok[:, kt, :], pb[:, :w], start=first_k, stop=last_k)
                        pr = dict(S=Sfn, w=w, bias=[nbias[:, 0:1], nbias[:, 1:2]], acc=acc, first=first_k,
                                  s_reads=[U(kU[0], kU[1], kblk), U(qU[0], qU[1], qb)],
                                  a_reads=[U('v', kt // 4)])
                        if last_k:
                            rr = {}
                            def pre_done(w=w, rr=rr):
                                for m in range(2):
                                    rr[m] = den_recip(m, w)
                            pr['pre_done'] = pre_done
                            def done(qb=qb, t0=t0, w=w, rr=rr):
                                tis = []
                                for m in range(2):
                                    ri = rr[m]
                                    r = tmpv(ri, F32)
                                    ti = tmp_next()
                                    t = tmpv(ti, F32)
                                    S.op('dve', lambda e, t=t, r=r, m=m: e.tensor_tensor(t[:, :w], psb[4 + m][:, :w], r[:, :w], ALU.mult),
                                         reads=[U('ps', 4 + m), U('tmp', ri)], writes=[U('tmp', ti)])
                                    tis.append(ti)
                                t0i, t1i = tis
                                ta, tb = tmpv(t0i, F32), tmpv(t1i, F32)
                                S.op('dve', lambda e: e.scalar_tensor_tensor(ta[:, :w], tb[:, :w], negl, ta[:, :w], ALU.mult, ALU.add),
                                     reads=[U('tmp', t0i), U('tmp', t1i), U('negl')], writes=[U('tmp', t0i)])
                                si = tmp_next()
                                sq = tmpv(si, BF16)
                                S.op('act', lambda e: e.activation(sq[:, :w], ta[:, :w], AF.Square),
                                     reads=[U('tmp', t0i)], writes=[U('tmp', si)])
                                bank = work_next()
                                S.op('pe', lambda e: e.matmul(psb[bank][:, :w], ones_bf[:], sq[:, :w], start=True, stop=True),
                                     reads=[U('tmp', si), U('ones')], writes=[U('ps', bank)])
                                S.op('act', lambda e: e.activation(tb[:, :w], psb[bank][:, :w], AF.Ln, bias=eps_sb[:], scale=1.0 / 128),
                                     reads=[U('ps', bank), U('eps')], writes=[U('tmp', t1i)])
                                S.op('act', lambda e: e.activation(tb[:, :w], tb[:, :w], AF.Exp, scale=-0.5),
                                     reads=[U('tmp', t1i)], writes=[U('tmp', t1i)])
                                S.op('dve', lambda e: e.scalar_tensor_tensor(oT[:, t0:t0 + w], ta[:, :w], sublnS, tb[:, :w], ALU.mult, ALU.mult),
                                     reads=[U('tmp', t0i), U('tmp', t1i), U('sublnS')], writes=[U('oT', qb)])
                            pr['done'] = done
                        pairs.append(pr)
                run_pairs(pairs)
                chk('tiles')
                wo_accumulate(l, sl[3], b, qblocks)
                chk('head0')

        NAG = _na_groups()

        def na_tables(head, slot):
            for i in range(2):
                src = bass.AP(zscr_t, head * 15 * 127, [[1, 64], [127, 15], [1, 64]])
                S.op('sp', lambda e, i=i, src=src: e.dma_start(out=na_stg[i * 64:(i + 1) * 64, i + 3:i + 18, :], in_=src),
                     reads=[U('zscr')], writes=[U('nastg', i)], dma=True)
            full = na_tab[slot][0]
            intr = na_tab[slot][1]
            stg_rev = bass.AP(na_stg.tensor, na_stg.offset + 63, [list(na_stg.ap[0]), [64, 22], [-1, 64]])
            cm_rev = bass.AP(cmask.tensor, cmask.offset + 63, [list(cmask.ap[0]), [0, 22], [-1, 64]])
            S.op('dve', lambda e: e.tensor_tensor(full[:], stg_rev, cm_rev, ALU.mult),
                 reads=[U('nastg', 0), U('nastg', 1), U('cmask')], writes=[U('natab', slot, 0)])
            for i in range(2):
                S.op('dve', lambda e, i=i: e.tensor_copy(intr[i * 64:(i + 1) * 64, i + 7:i + 15, :], full[i * 64:(i + 1) * 64, i + 7:i + 15, :]),
                     reads=[U('natab', slot, 0)], writes=[U('natab', slot, 1, i)])

        def na_attention_layer(l, b):
            qblocks = [1, 2, 3, 4]
            for g in range(8):
                par = g % 2
                qT_, kT_ = big[par * 2], big[par * 2 + 1]
                qU, kU = ('B', par * 2), ('B', par * 2 + 1)
                sl = w_alloc(4)
                load_w(wqkv_r[l, 0, g], sl[0])
                load_w(wqkv_r[l, 1, g], sl[1])
                load_w(wqkv_r[l, 2, g], sl[2])
                load_w(wo_r[l, g], sl[3])
                project_T(sl[0], qT_, qU, qblocks, rope=False)
                project_T(sl[1], kT_, kU, [0, 1, 2, 3, 4], rope=False)
                project_v(sl[2])
                nbias = small[:, 16 + par * 2:18 + par * 2]
                shift_bounds(qT_, kT_, qU, kU, qblocks, nbias)
                for hh in range(2):
                    na_tables(2 * g + hh, hh)
                pairs = []
                for (r0, nr, kind, chs) in NAG:
                    w = nr * 64
                    t0 = LCTX + r0 * 64
                    qbs = sorted(set([1 + (r0 * 64) // 512, 1 + ((r0 + nr) * 64 - 1) // 512]))
                    klist = [('ctx', 0), ('ctx', 1)] + [('band', ch) for ch in chs]
                    for ki, (kk, ch) in enumerate(klist):
                        first_k = (ki == 0)
                        last_k = (ki == len(klist) - 1)
                        kt = ch if kk == 'ctx' else 2 + ch
                        kblk = 0 if kt < 2 else 1 + (kt - 2) // 4
                        def Sfn(e, ba, bb, kt=kt, t0=t0, w=w):
                            e.matmul(ba[:, :w], kT_[0:64, kt * 128:(kt + 1) * 128], qT_[0:64, t0:t0 + w], start=True, stop=True)
                            return e.matmul(bb[:, :w], kT_[64:128, kt * 128:(kt + 1) * 128], qT_[64:128, t0:t0 + w], start=True, stop=True)
                        def acc(e, pa, pb, kt=kt, w=w, first_k=first_k, last_k=last_k):
                            e.matmul(psb[4][:, :w], vtok[:, kt, :], pa[:, :w], start=first_k, stop=last_k)
                            return e.matmul(psb[5][:, :w], vtok[:, kt, :], pb[:, :w], start=first_k, stop=last_k)
                        pr = dict(S=Sfn, w=w, bias=[nbias[:, 0:1], nbias[:, 1:2]], acc=acc, first=first_k,
                                  s_reads=[U(kU[0], kU[1], kblk)] + [U(qU[0], qU[1], qb) for qb in qbs],
                                  a_reads=[U('v', kt // 4)])
                        if kk == 'band':
                            Dd = 2 * ch - r0
                            u0 = 7 - Dd + 3
                            assert 0 <= u0 and u0 + nr <= 22, (u0, nr)
                            def post(hf, pt, pi, u0=u0, nr=nr, w=w, kind=kind):
                                tabv = na_tab[hf][0 if kind == 'full' else 1]
                                rd = [U('natab', hf, 0)] if kind == 'full' else [U('natab', hf, 1, 0), U('natab', hf, 1, 1)]
                                S.op('dve', lambda e: e.tensor_tensor(pt[:, :w], pt[:, :w],
                                                                      tabv[:, u0:u0 + nr, :].rearrange("p u c -> p (u c)"), ALU.mult),
                                     reads=[U('pt', pi)] + rd, writes=[U('pt', pi)])
                            pr['post'] = post
                        if last_k:
                            rr = {}
                            def pre_done(w=w, rr=rr):
                                for hh in range(2):
                                    rr[hh] = den_recip(hh, w, rows=slice(hh * 64, (hh + 1) * 64))
                            pr['pre_done'] = pre_done
                            def done(t0=t0, w=w, qbs=qbs, rr=rr):
                                for hh in range(2):
                                    hs = slice(hh * 64, (hh + 1) * 64)
                                    ri = rr[hh]
                                    r = tmpv(ri, F32)
                                    S.op('dve', lambda e, hh=hh, hs=hs, r=r: e.tensor_tensor(oT[hs, t0:t0 + w], psb[4 + hh][hs, :w], r[hs, :w], ALU.mult),
                                         reads=[U('ps', 4 + hh), U('tmp', ri)] + [U('oT', qb) for qb in qbs],
                                         writes=[U('oT', qb) for qb in qbs])
                            pr['done'] = done
                        pairs.append(pr)
                run_pairs(pairs)
                wo_accumulate(l, sl[3], b, qblocks)

        def ffn_layer(l, b, blocks):
            norm_modulate(l, 1, b, blocks)
            JG = 4
            for j0 in range(0, NJ, JG):
                js = list(range(j0, min(NJ, j0 + JG)))
                wslots = {}
                for jj, j in enumerate(js):
                    sl = w_alloc(3)
                    wslots[j] = sl
                    S.op('pool', lambda e, sl=sl, j=j: e.dma_start(
                        out=wring[:, sl[0] * 1024:(sl[0] + 2) * 1024], in_=w1_r[l, j]),
                        writes=[U('w', sl[0]), U('w', sl[1])], dma=True)
                    load_w(w2_r[l, j], sl[2])
                    wg = wr[sl[0]].rearrange("p (k j) -> p k j", k=8)
                    wu = wr[sl[1]].rearrange("p (k j) -> p k j", k=8)
                    for blk in blocks:
                        t0, w = BLK[blk]
                        bg = work_next()
                        bu = work_next()
                        def mm(e, wg=wg, wu=wu, bg=bg, bu=bu, t0=t0, w=w):
                            for kc in range(8):
                                e.matmul(psb[bg][:, :w], wg[:, kc, :], xn[:, kc, t0:t0 + w], start=(kc == 0), stop=(kc == 7))
                            for kc in range(8):
                                r = e.matmul(psb[bu][:, :w], wu[:, kc, :], xn[:, kc, t0:t0 + w], start=(kc == 0), stop=(kc == 7))
                            return r
                        S.op('pe', mm, reads=[U('w', sl[0]), U('w', sl[1])] + [U('xn', c, blk) for c in range(8)],
                             writes=[U('ps', bg), U('ps', bu)])
                        ti = tmp_next()
                        sg = tmpv(ti, F32)
                        S.op('act', lambda e, sg=sg, bg=bg, w=w: e.activation(sg[:, :w], psb[bg][:, :w], AF.Silu),
                             reads=[U('ps', bg)], writes=[U('tmp', ti)])
                        S.op('dve', lambda e, sg=sg, bu=bu, jj=jj, t0=t0, w=w: e.tensor_tensor(big[jj][:, t0:t0 + w], sg[:, :w], psb[bu][:, :w], ALU.mult),
                             reads=[U('tmp', ti), U('ps', bu)], writes=[U('B', jj, blk)])
                for oc in range(8):
                    for blk in blocks:
                        t0, w = BLK[blk]
                        jv = 2 if blk == 0 else b
                        bank = work_next()
                        def mm2(e, bank=bank, oc=oc, t0=t0, w=w):
                            for jj, j in enumerate(js):
                                r = e.matmul(psb[bank][:, :w], wr[wslots[j][2]][:, oc * 128:(oc + 1) * 128], big[jj][:, t0:t0 + w],
                                             start=(jj == 0), stop=(jj == len(js) - 1))
                            return r
                        S.op('pe', mm2, reads=[U('w', wslots[j][2]) for j in js] + [U('B', jj, blk) for jj in range(len(js))],
                             writes=[U('ps', bank)])
                        S.op('dve', lambda e, bank=bank, oc=oc, t0=t0, w=w, jv=jv: e.scalar_tensor_tensor(
                            hT[:, oc, t0:t0 + w], psb[bank][:, :w], modT[:, l, 40 + oc, jv:jv + 1], hT[:, oc, t0:t0 + w], ALU.mult, ALU.add),
                            reads=[U('ps', bank), U('modT'), U('h', oc, blk)], writes=[U('h', oc, blk)])

        for b in range(nb):
            for c in range(8):
                for blk in range(5):
                    pass
                S.op('sp', lambda e, b=b, c=c: e.dma_start(out=hT[:, c, :], in_=xT[b, c * 128:(c + 1) * 128, :]),
                     writes=[U('h', c, blk) for blk in range(5)], dma=True)
            for l in layers:
              try:
                last = (l == 1)
                blocks = [1, 2, 3, 4] if last else [0, 1, 2, 3, 4]
                if stop == 'setup':
                    break
                norm_modulate(l, 0, b, [0, 1, 2, 3, 4])
                if stop == 'norm':
                    break
                if l == 0:
                    S.op('sp', lambda e: e.dma_start(out=tab[:].bitcast(F32).rearrange("p (a n) -> p a n", a=2), in_=ropet),
                         writes=[U('tab')] + [U('natab', s, 0) for s in range(2)] +
                                [U('natab', s, 1, i) for s in range(2) for i in range(2)] + [U('nastg', i) for i in range(2)], dma=True)
                    diff_attention_layer(l, b, need_ctx=not last)
                else:
                    S.op('pool', lambda e: e.memset(tab[:].bitcast(BF16), 0.0), reads=[U('tab')],
                         writes=[U('tab')] + [U('nastg', i) for i in range(2)] + [U('natab', s, 0) for s in range(2)] +
                                [U('natab', s, 1, i) for s in range(2) for i in range(2)])
                    na_attention_layer(l, b)
                if stop == 'attn':
                    break
                ffn_layer(l, b, blocks)
              except _Stop:
                break
            if final:
                for blk in [1, 2, 3, 4]:
                    t0, w = BLK[blk]
                    ri = rms_rstd(None, blk, 1.0 / D)
                    r = rstd_buf[ri]
                    for c in range(8):
                        ti = tmp_next()
                        t = tmpv(ti, F32)
                        S.op('dve', lambda e, t=t, c=c, t0=t0, w=w, r=r: e.scalar_tensor_tensor(
                            t[:, :w], hT[:, c, t0:t0 + w], nrm_sb[:, 4, c:c + 1], r[:, :w], ALU.mult, ALU.mult),
                            reads=[U('h', c, blk), U('rstd', ri), U('nrm')], writes=[U('tmp', ti)])
                        S.op('sp', lambda e, t=t, c=c, b=b, t0=t0, w=w: e.dma_start(
                            out=outT[b, c * 128:(c + 1) * 128, t0 - LCTX:t0 - LCTX + w], in_=t[:, :w]),
                            reads=[U('tmp', ti)], dma=True)
            else:
                for c in range(8):
                    S.op('sp', lambda e, b=b, c=c: e.dma_start(out=outT[b, c * 128:(c + 1) * 128, :], in_=hT[:, c, :]),
                         reads=[U('h', c, blk) for blk in range(5)], dma=True)
        S.emit()
    return nc


def _vecT(v):
    v = np.asarray(v, np.float32)
    return np.ascontiguousarray(np.moveaxis(v.reshape(v.shape[:-1] + (8, 128)), -1, 0))


def prepare_shared(inputs):
    f = lambda k: np.asarray(inputs[k], np.float32)
    sh = {}
    ada_w = f("ada_w")
    sh["ada_r"] = np.ascontiguousarray(ada_w.reshape(2, 8, 128, 6, 1024).transpose(0, 3, 1, 2, 4))
    sh["ada_bT"] = np.ascontiguousarray(f("ada_b").reshape(2, 48, 128).transpose(2, 0, 1))
    nrm = np.stack([f("norm_mix")[0], f("norm_mix")[1], f("norm_ffn")[0], f("norm_ffn")[1], f("norm_final")], 0)
    sh["nrm"] = np.ascontiguousarray(nrm.reshape(5, 8, 128).transpose(2, 0, 1))
    wqkv = np.stack([f("da_wqkv")[0], f("na_wqkv")[0]], 0)
    sh["wqkv_r"] = np.ascontiguousarray(wqkv.reshape(2, 8, 128, 3, 8, 128).transpose(0, 3, 4, 2, 1, 5)).reshape(2, 3, 8, 128, 1024)
    wo = np.stack([f("da_wo")[0], f("na_wo")[0]], 0)
    sh["wo_r"] = np.ascontiguousarray(wo.reshape(2, 8, 128, 1024))
    w1 = f("ffn_w_gate_up")
    sh["w1_r"] = np.ascontiguousarray(w1.reshape(2, 8, 128, 2, NJ, 128).transpose(0, 4, 2, 3, 1, 5)).reshape(2, NJ, 128, 2048)
    sh["w2_r"] = np.ascontiguousarray(f("ffn_w_down").reshape(2, NJ, 128, 1024))
    sh["lamv"] = np.ascontiguousarray(np.concatenate([f("da_lambda_q1")[0], f("da_lambda_k1")[0],
                                                      f("da_lambda_q2")[0], f("da_lambda_k2")[0]])[None, :])
    sh["subln"] = np.ascontiguousarray(f("da_subln")[0][:, None])
    rp = f("na_rpb")[0][:, ::-1, :]
    sh["rpb"] = np.ascontiguousarray(rp.reshape(120, 2, 31))
    sh["ropet"] = _rope_tables()
    sh["rotm"] = _rot_matrix()
    sh["cmask"] = _na_colmask_rev()
    return sh


def prepare_core(inputs, core, nb=NB):
    x = np.asarray(inputs["x"], np.float32)
    ctx = np.asarray(inputs["ctx"], np.float32)
    c = np.asarray(inputs["c"], np.float32)
    c_ctx = np.asarray(inputs["c_ctx"], np.float32)
    bs = slice(core * nb, (core + 1) * nb)
    xc = np.concatenate([ctx[bs], x[bs]], axis=1)
    m = {"xT": np.ascontiguousarray(xc.transpose(0, 2, 1))}
    cols = [c[core * nb + i] for i in range(nb)]
    while len(cols) < 2:
        cols.append(cols[-1])
    cols.append(c_ctx)
    cm = np.stack(cols, axis=-1)
    m["cT"] = np.ascontiguousarray(cm.reshape(8, 128, 3).transpose(1, 0, 2))
    return m


_PROG = {}


def kernel(**inputs):
    if 'full' not in _PROG:
        _PROG['full'] = build_program()
    nc = _PROG['full']
    sh = prepare_shared(inputs)
    in_maps = []
    for core in range(NCORES):
        m = dict(sh)
        m.update(prepare_core(inputs, core))
        in_maps.append(m)
    res = run_bass_kernel_spmd(nc, in_maps, core_ids=list(range(NCORES)))
    outs = [np.asarray(r["outT"]) for r in res.results]
    o = np.concatenate(outs, axis=0)
    return np.ascontiguousarray(o.transpose(0, 2, 1)).astype(np.float32)
```

```python
import contextlib
import numpy as np
import concourse.bass as bass
import concourse.mybir as mybir
from concourse.bass_utils import run_bass_kernel_spmd

F32 = mybir.dt.float32
BF16 = mybir.dt.bfloat16
U8 = mybir.dt.uint8
AF = mybir.ActivationFunctionType
ALU = mybir.AluOpType
AX = mybir.AxisListType

ENGS = ['pe', 'act', 'dve', 'pool', 'sp']

D = 1024
NTOK = 2304
LCTX = 256
NLAT = 2048
DFF = 2816
NJ = DFF // 128
EPS = 1e-6
NCORES = 8
NB = 2
import os
ROPE_ENG = os.environ.get('ROPE_ENG', 'dve')
FAST_RECIP = os.environ.get('FAST_RECIP', '0') == '1'


import types


def freeze(fn):
    if getattr(fn, '__closure__', None) is None:
        return fn
    cells = []
    for c in fn.__closure__:
        try:
            v = c.cell_contents
            if isinstance(v, types.FunctionType) and v is not fn:
                v = freeze(v)
            cells.append(types.CellType(v))
        except ValueError:
            cells.append(c)
    g = types.FunctionType(fn.__code__, fn.__globals__, fn.__name__, fn.__defaults__, tuple(cells))
    g.__kwdefaults__ = fn.__kwdefaults__
    return g


class _Op:
    __slots__ = ('id', 'eng', 'fn', 'chan', 'seq', 'waits', 'signal', 'clock', 'is_dma')


class Sched:
    def __init__(self, nc, dma_slots=None):
        self.nc = nc
        self.ops = []
        self.eng_ops = {e: [] for e in ENGS}
        self.units = {}
        self.eng_clock = {e: {} for e in ENGS}
        self.comp_seq = {e: 0 for e in ENGS}
        self.chan_ops = {}
        self.dma_slots = dma_slots or {'sp': 8, 'pool': 10, 'act': 4}
        self.dma_rr = {q: 0 for q in self.dma_slots}
        self.dma_cnt = {}
        self.dma_last = {}

    def op(self, eng, fn, reads=(), writes=(), dma=False):
        o = _Op()
        o.id = len(self.ops)
        o.eng = eng
        o.fn = freeze(fn)
        o.is_dma = dma
        o.signal = False
        deps = set()
        ps_reads = [u for u in reads if u[0] == 'ps']
        if ps_reads:
            reads = [u for u in reads if u[0] != 'ps']
            writes = list(writes) + [u for u in ps_reads if u not in writes]
        for u in reads:
            st = self.units.get(u)
            if st is not None and st[0] is not None:
                deps.add(st[0])
        for u in writes:
            st = self.units.get(u)
            if st is not None:
                if st[0] is not None:
                    deps.add(st[0])
                deps.update(st[1])
        if dma:
            q = eng
            slot = self.dma_rr[q]
            self.dma_rr[q] = (slot + 1) % self.dma_slots[q]
            chan = ('dma', q, slot)
            prev = self.dma_last.get(chan)
            if prev is not None:
                deps.add(prev)
            self.dma_cnt[chan] = self.dma_cnt.get(chan, 0) + 1
            o.chan = chan
            o.seq = self.dma_cnt[chan]
            self.dma_last[chan] = o.id
        else:
            self.comp_seq[eng] += 1
            o.chan = eng
            o.seq = self.comp_seq[eng]
        clk = self.eng_clock[eng]
        waits = {}
        for d in sorted(deps, reverse=True):
            dop = self.ops[d]
            if dop.chan == 'pe' and eng == 'pe' and not dma:
                continue
            if clk.get(dop.chan, 0) >= dop.seq:
                continue
            if waits.get(dop.chan, 0) < dop.seq:
                waits[dop.chan] = dop.seq
            for c, s in dop.clock.items():
                if clk.get(c, 0) < s:
                    clk[c] = s
        o.waits = list(waits.items())
        for c, s in o.waits:
            self.chan_ops[c][s - 1].signal = True
        o.clock = dict(clk)
        o.clock[o.chan] = o.seq
        self.chan_ops.setdefault(o.chan, []).append(o)
        self.ops.append(o)
        self.eng_ops[eng].append(o)
        for u in reads:
            st = self.units.get(u)
            if st is None:
                st = [None, []]
                self.units[u] = st
            st[1].append(o.id)
        for u in writes:
            self.units[u] = [o.id, []]
        return o

    def emit(self):
        nc = self.nc
        chans = list(self.chan_ops.keys())
        with contextlib.ExitStack() as es:
            sems = {}
            for c in chans:
                nm = 's_' + ('_'.join(str(x) for x in c) if isinstance(c, tuple) else c)
                sems[c] = es.enter_context(nc.semaphore(nm))
            sigcount = {}
            for c, lst in self.chan_ops.items():
                if isinstance(c, tuple):
                    continue
                n = 0
                arr = []
                for o in lst:
                    if o.signal:
                        n += 1
                    arr.append(n)
                sigcount[c] = arr

            def wval(c, s):
                if isinstance(c, tuple):
                    return 16 * s
                return sigcount[c][s - 1]

            block = es.enter_context(nc.Block())
            engmap = {'pe': block.tensor, 'act': block.scalar, 'dve': block.vector,
                      'pool': block.gpsimd, 'sp': block.sync}
            for e in ENGS:
                lst = self.eng_ops[e]
                if not lst:
                    continue
                fin = [(c, self.dma_cnt[c]) for c in chans if isinstance(c, tuple) and c[1] == e]

                def body(engine, lst=lst, fin=fin):
                    for o in lst:
                        for c, s in o.waits:
                            engine.wait_ge(sems[c], wval(c, s))
                        ins = o.fn(engine)
                        if o.is_dma:
                            ins.then_inc(sems[o.chan], 16)
                        elif o.signal:
                            ins.then_inc(sems[o.chan], 1)
                    for c, n in fin:
                        engine.wait_ge(sems[c], 16 * n)
                engmap[e](body)


def _rope_tables():
    t = np.arange(NLAT)
    row = (t // 64).astype(np.float32)
    col = (t % 64).astype(np.float32)
    half = 32
    freqs = (1.0 / (np.float32(10000.0) ** (np.arange(0, half, 2, dtype=np.float32) / np.float32(half)))).astype(np.float32)
    ar = row[:, None] * freqs
    ac = col[:, None] * freqs
    ang = np.concatenate([ar, ar, ac, ac], axis=-1).astype(np.float32)
    cos = np.cos(ang).astype(np.float32).T
    sin = np.sin(ang).astype(np.float32).T
    cs = np.stack([np.concatenate([cos, cos], 0), np.concatenate([sin, sin], 0)], axis=1)
    return np.ascontiguousarray(cs)


def _rot_matrix():
    R = np.zeros((128, 128), np.float32)
    for m in range(128):
        d = m % 32
        base = m - d
        if d < 16:
            R[base + d + 16, m] = -1.0
        else:
            R[base + d - 16, m] = 1.0
    return R


def _na_colmask_rev():
    qc = np.arange(64)
    cs = np.clip(qc - 8, 0, 48)
    m = np.zeros((64, 64), np.float32)
    for c in range(64):
        m[cs[c]:cs[c] + 16, c] = 1.0
    mrev = m[:, ::-1]
    return np.ascontiguousarray(np.concatenate([mrev, mrev], 0))


def _na_groups():
    groups = []
    def chunks(lo_row, hi_row):
        return list(range(lo_row // 2, hi_row // 2 + 1))
    groups.append((0, 4, 'full', chunks(0, 7)))
    groups.append((4, 4, 'int', chunks(0, 10)))
    groups.append((8, 8, 'int', chunks(4, 18)))
    groups.append((16, 8, 'int', chunks(12, 26)))
    groups.append((24, 5, 'int', chunks(20, 31)))
    groups.append((29, 3, 'full', chunks(24, 31)))
    return groups


class _Stop(Exception):
    pass


def build_program(nb=NB, layers=(0, 1), first=True, final=True, dbg=False, stop=None):
    def chk(stage):
        if stop == stage:
            raise _Stop()
    nc = bass.Bass("TRN2", target_bir_lowering=False)

    def din(name, shape, dt=F32):
        return nc.dram_tensor(name, list(shape), dt, kind="ExternalInput").ap()

    xT = din("xT", [nb, D, NTOK])
    cT = din("cT", [128, 8, 3])
    ada_r = din("ada_r", [2, 6, 8, 128, 1024])
    ada_bT = din("ada_bT", [128, 2, 48])
    nrm = din("nrm", [128, 5, 8])
    wqkv_r = din("wqkv_r", [2, 3, 8, 128, 1024])
    wo_r = din("wo_r", [2, 8, 128, 1024])
    w1_r = din("w1_r", [2, NJ, 128, 2048])
    w2_r = din("w2_r", [2, NJ, 128, 1024])
    lamv = din("lamv", [1, 256])
    subln = din("subln", [128, 1])
    rpb = din("rpb", [120, 2, 31])
    ropet = din("ropet", [128, 2, NLAT])
    rotm_d = din("rotm", [128, 128])
    cmask_d = din("cmask", [128, 64])
    outT = nc.dram_tensor("outT", [nb, D, NLAT if final else NTOK], F32, kind="ExternalOutput").ap()
    zscr_t = nc.dram_tensor("zscr", [240, 127], BF16)
    zscr = zscr_t.ap()

    S = Sched(nc)
    es = contextlib.ExitStack()
    with es:
        ARENA = 212000
        arena = nc.alloc_sbuf_tensor("arena", [128, ARENA], U8)
        off = [0]

        def carve(nbytes, dt, pattern=None, **kw):
            a = off[0]
            off[0] += (nbytes + 31) // 32 * 32
            assert off[0] <= ARENA, off[0]
            v = arena[:, a:a + nbytes].bitcast(dt)
            if pattern:
                v = v.rearrange(pattern, **kw)
            return v

        hT = carve(8 * NTOK * 4, F32, "p (c t) -> p c t", c=8)
        xn = carve(8 * NTOK * 2, BF16, "p (c t) -> p c t", c=8)
        big = [carve(NTOK * 2, BF16) for _ in range(4)]
        vtok = carve(18 * 128 * 2, BF16, "p (t e) -> p t e", t=18)
        oT = carve(NTOK * 2, BF16)
        NPT = 6
        PT = [carve(512 * 2, BF16) for _ in range(NPT)]
        NW = 12
        wring = carve(NW * 2048, BF16)
        wr = [wring[:, i * 1024:(i + 1) * 1024] for i in range(NW)]
        rstd_buf = [carve(2048, F32) for _ in range(2)]
        dacc = [carve(2048, F32) for _ in range(2)]
        lamtmp = carve(160 * 4, F32)
        tab = carve(16384, U8)
        NTMP = 6
        tmp = [carve(2048, U8) for _ in range(NTMP)]
        ones_bf = carve(128 * 2, BF16)
        rotm = carve(128 * 2, BF16)
        ones_f = carve(128 * 4, F32)
        modT = carve(2 * 48 * 3 * 4, F32, "p (l c j) -> p l c j", l=2, c=48)
        Gv = carve(2 * 2 * 3 * 8 * 4, F32, "p (l m j c) -> p l m j c", l=2, m=2, j=3)
        nrm_sb = carve(5 * 8 * 4, F32, "p (n c) -> p n c", n=5)
        adab_sb = carve(2 * 48 * 4, F32, "p (l c) -> p l c", l=2)
        c_sb = carve(8 * 3 * 4, F32, "p (k j) -> p k j", k=8)
        sc_bf = carve(8 * 3 * 2, BF16, "p (k j) -> p k j", k=8)
        small = carve(64 * 4, F32)
        lam_sb = carve(256 * 4, F32)
        subln_sb = carve(4, F32)
        eps_sb = carve(4, F32)
        mx = carve(2 * 2 * 5 * 4, F32, "p (a m b) -> p a m b", a=2, m=2)
        cmask = carve(64 * 2, BF16)
        ps_t = nc.alloc_psum_tensor("ps", [128, 8, 512], F32)
        psb = [ps_t[:, k, :] for k in range(8)]

        cosT = tab[:, 0:8192].bitcast(F32)
        sinT = tab[:, 8192:16384].bitcast(F32)
        na_stg = tab[:, 0:2816].bitcast(BF16).rearrange("p (u c) -> p u c", u=22)
        na_tab = [[tab[:, 2816 + (s * 2 + v) * 2816: 2816 + (s * 2 + v + 1) * 2816].bitcast(BF16)
                   .rearrange("p (u c) -> p u c", u=22) for v in range(2)] for s in range(2)]

        def tmpv(i, dt, w=512):
            nbytes = w * (4 if dt == F32 else 2)
            return tmp[i][:, 0:nbytes].bitcast(dt)

        st = {'tmp': 0, 'pt': 0, 'w': 0, 'wk': 0, 'rs': 0, 'pw': 0, 'wo': 0}

        def tmp_next():
            i = st['tmp']
            st['tmp'] = (i + 1) % NTMP
            return i

        def pt_next():
            i = st['pt']
            st['pt'] = (i + 1) % NPT
            return i

        def w_alloc(n):
            p = st['w']
            if p + n > NW:
                p = 0
            st['w'] = p + n
            return list(range(p, p + n))

        WORK = [0, 1, 2, 3]

        def work_next():
            i = st['wk']
            st['wk'] = (i + 1) % len(WORK)
            return WORK[i]

        def U(*a):
            return a

        BLK = [(0, 256)] + [(256 + 512 * i, 512) for i in range(4)]

        S.op('pool', lambda e: e.memset(ones_bf[:], 1.0), writes=[U('ones')])
        S.op('pool', lambda e: e.memset(ones_f[:], 1.0), writes=[U('onesf')])
        S.op('pool', lambda e: e.memset(eps_sb[:], EPS), writes=[U('eps')])
        S.op('pool', lambda e: e.dma_start(out=rotm[:], in_=rotm_d), writes=[U('rotm')], dma=True)
        S.op('pool', lambda e: e.dma_start(out=cmask[:], in_=cmask_d), writes=[U('cmask')], dma=True)
        S.op('sp', lambda e: e.dma_start(out=nrm_sb[:], in_=nrm), writes=[U('nrm')], dma=True)
        S.op('sp', lambda e: e.dma_start(out=adab_sb[:], in_=ada_bT), writes=[U('adab')], dma=True)
        S.op('sp', lambda e: e.dma_start(out=c_sb[:], in_=cT), writes=[U('c')], dma=True)
        S.op('sp', lambda e: e.dma_start(out=lam_sb[0:1, :], in_=lamv), writes=[U('lamrow')], dma=True)
        S.op('sp', lambda e: e.dma_start(out=subln_sb[:], in_=subln), writes=[U('subln')], dma=True)

        S.op('act', lambda e: e.activation(sc_bf[:], c_sb[:], AF.Silu), reads=[U('c')], writes=[U('sc')])

        if first:
            for l in layers:
                for grp in range(6):
                    slots = []
                    for kc in range(8):
                        s = w_alloc(1)[0]
                        slots.append(s)
                        S.op('pool', lambda e, s=s, l=l, grp=grp, kc=kc: e.dma_start(out=wr[s][:], in_=ada_r[l, grp, kc]),
                             writes=[U('w', s)], dma=True)
                        for o in range(8):
                            S.op('pe', lambda e, s=s, o=o, kc=kc: e.matmul(psb[o][:, 0:3], wr[s][:, o * 128:(o + 1) * 128],
                                                                            sc_bf[:, kc, :], start=(kc == 0), stop=(kc == 7)),
                                 reads=[U('w', s), U('sc')], writes=[U('ps', o)])
                    for o in range(8):
                        ch = grp * 8 + o
                        S.op('dve', lambda e, o=o, l=l, ch=ch: e.tensor_scalar(modT[:, l, ch, :], psb[o][:, 0:3],
                                                                                   adab_sb[:, l, ch:ch + 1], None, ALU.add),
                             reads=[U('ps', o), U('adab')], writes=[U('modT')])
            for l in layers:
                for mf in range(2):
                    for j in range(3):
                        sc_base = 8 if mf == 0 else 32
                        nidx = l if mf == 0 else 2 + l
                        S.op('dve', lambda e, l=l, mf=mf, j=j, sc_base=sc_base, nidx=nidx: e.scalar_tensor_tensor(
                            Gv[:, l, mf, j, :], modT[:, l, sc_base:sc_base + 8, j], 1.0, nrm_sb[:, nidx, :], ALU.add, ALU.mult),
                            reads=[U('modT'), U('nrm')], writes=[U('Gv')])

        LAM_INIT = 0.8 - 0.6 * float(np.exp(-0.3 * 0))
        negl = small[:, 0:1]
        sublnS = small[:, 1:2]
        if 0 in layers:
            lr = lam_sb[0:1, :]
            prod = lamtmp
            S.op('dve', lambda e: e.tensor_tensor(prod[0:1, 0:64], lr[:, 0:64], lr[:, 64:128], ALU.mult),
                 reads=[U('lamrow')], writes=[U('lamtmp')])
            S.op('dve', lambda e: e.tensor_tensor(prod[0:1, 64:128], lr[:, 128:192], lr[:, 192:256], ALU.mult),
                 reads=[U('lamrow'), U('lamtmp')], writes=[U('lamtmp')])
            S.op('dve', lambda e: e.tensor_reduce(prod[0:1, 128:130], prod[0:1, 0:128].rearrange("p (a b) -> p a b", a=2),
                                                  AX.X, ALU.add), reads=[U('lamtmp')], writes=[U('lamtmp')])
            S.op('act', lambda e: e.activation(prod[0:1, 130:132], prod[0:1, 128:130], AF.Exp),
                 reads=[U('lamtmp')], writes=[U('lamtmp')])
            S.op('dve', lambda e: e.tensor_tensor(prod[0:1, 132:133], prod[0:1, 131:132], prod[0:1, 130:131], ALU.subtract),
                 reads=[U('lamtmp')], writes=[U('lamtmp')])
            S.op('dve', lambda e: e.tensor_scalar(prod[0:1, 133:134], prod[0:1, 132:133], -LAM_INIT, None, ALU.add),
                 reads=[U('lamtmp')], writes=[U('lamtmp')])
            S.op('pe', lambda e: e.matmul(psb[0][:, 0:1], ones_f[0:1, :], prod[0:1, 133:134], start=True, stop=True),
                 reads=[U('lamtmp'), U('onesf')], writes=[U('ps', 0)])
            S.op('dve', lambda e: e.tensor_copy(negl, psb[0][:, 0:1]), reads=[U('ps', 0)], writes=[U('negl')])
            S.op('dve', lambda e: e.tensor_scalar(sublnS, subln_sb[:], 1.0 - LAM_INIT, None, ALU.mult),
                 reads=[U('subln')], writes=[U('sublnS')])

        if 1 in layers:
            zt = tmpv(1, BF16, 512)
            rp = tmpv(2, F32, 512)
            zv = zt[0:120, 0:254].rearrange("p (a s) -> p a s", a=2)
            S.op('sp', lambda e: e.dma_start(out=rp[0:120, 0:62].rearrange("p (a s) -> p a s", a=2), in_=rpb),
                 writes=[U('tmp', 2)], dma=True)
            S.op('pool', lambda e: e.memset(zt[0:120, 0:254], 0.0), writes=[U('tmp', 1)])
            S.op('act', lambda e: e.activation(zv[:, :, 48:79], rp[0:120, 0:62].rearrange("p (a s) -> p a s", a=2), AF.Exp),
                 reads=[U('tmp', 2), U('tmp', 1)], writes=[U('tmp', 1)])
            S.op('sp', lambda e: e.dma_start(out=zscr.rearrange("(p a) s -> p a s", a=2), in_=zv),
                 reads=[U('tmp', 1)], writes=[U('zscr')], dma=True)

        def rms_rstd(src_blocks, blk, inv_n, nchunk_parts=128):
            t0, w = BLK[blk]
            bank = work_next()
            for c in range(8):
                ti = tmp_next()
                sq = tmpv(ti, BF16)
                S.op('act', lambda e, sq=sq, c=c: e.activation(sq[:, :w], hT[:, c, t0:t0 + w], AF.Square),
                     reads=[U('h', c, blk)], writes=[U('tmp', ti)])
                S.op('pe', lambda e, sq=sq, c=c: e.matmul(psb[bank][:, :w], ones_bf[:], sq[:, :w], start=(c == 0), stop=(c == 7)),
                     reads=[U('tmp', ti), U('ones')], writes=[U('ps', bank)])
            ri = st['rs']
            st['rs'] = 1 - ri
            r = rstd_buf[ri]
            S.op('act', lambda e: e.activation(r[:, :w], psb[bank][:, :w], AF.Ln, bias=eps_sb[:], scale=inv_n),
                 reads=[U('ps', bank), U('eps')], writes=[U('rstd', ri)])
            S.op('act', lambda e: e.activation(r[:, :w], r[:, :w], AF.Exp, scale=-0.5),
                 reads=[U('rstd', ri)], writes=[U('rstd', ri)])
            return ri

        def norm_modulate(l, mf, b, blocks):
            for blk in blocks:
                t0, w = BLK[blk]
                j = 2 if blk == 0 else b
                ri = rms_rstd(None, blk, 1.0 / D)
                r = rstd_buf[ri]
                sh_base = 0 if mf == 0 else 24
                for c in range(8):
                    ti = tmp_next()
                    t = tmpv(ti, F32)
                    S.op('dve', lambda e, t=t, c=c: e.tensor_tensor(t[:, :w], hT[:, c, t0:t0 + w], r[:, :w], ALU.mult),
                         reads=[U('h', c, blk), U('rstd', ri)], writes=[U('tmp', ti)])
                    S.op('act', lambda e, t=t, c=c, j=j: e.activation(xn[:, c, t0:t0 + w], t[:, :w], AF.Identity,
                                                                      bias=modT[:, l, sh_base + c, j:j + 1],
                                                                      scale=Gv[:, l, mf, j, c:c + 1]),
                         reads=[U('tmp', ti), U('modT'), U('Gv')], writes=[U('xn', c, blk)])

        def load_w(src_ap, slot):
            S.op('pool', lambda e: e.dma_start(out=wr[slot][:], in_=src_ap), writes=[U('w', slot)], dma=True)

        def project_T(wslot, dst_buf, dst_unit, blocks, rope):
            wv = wr[wslot].rearrange("p (k j) -> p k j", k=8)
            for blk in blocks:
                t0, w = BLK[blk]
                bank = work_next()
                def mm(e, bank=bank, t0=t0, w=w):
                    for kc in range(8):
                        r = e.matmul(psb[bank][:, :w], wv[:, kc, :], xn[:, kc, t0:t0 + w], start=(kc == 0), stop=(kc == 7))
                    return r
                S.op('pe', mm, reads=[U('w', wslot)] + [U('xn', c, blk) for c in range(8)], writes=[U('ps', bank)])
                if rope and blk > 0:
                    ti = tmp_next()
                    qpre = tmpv(ti, BF16)
                    S.op('act', lambda e, qpre=qpre, bank=bank, w=w: e.activation(qpre[:, :w], psb[bank][:, :w], AF.Copy),
                         reads=[U('ps', bank)], writes=[U('tmp', ti)])
                    bank2 = work_next()
                    S.op('pe', lambda e, qpre=qpre, bank2=bank2, w=w: e.matmul(psb[bank2][:, :w], rotm[:], qpre[:, :w], start=True, stop=True),
                         reads=[U('tmp', ti), U('rotm')], writes=[U('ps', bank2)])
                    l0 = t0 - LCTX
                    t1i = tmp_next()
                    t1 = tmpv(t1i, F32)
                    S.op('dve', lambda e, t1=t1, bank=bank, l0=l0, w=w: e.tensor_tensor(t1[:, :w], psb[bank][:, :w], cosT[:, l0:l0 + w], ALU.mult),
                         reads=[U('ps', bank), U('tab')], writes=[U('tmp', t1i)])
                    t2i = tmp_next()
                    t2 = tmpv(t2i, F32)
                    S.op('dve', lambda e, t2=t2, bank2=bank2, l0=l0, w=w: e.tensor_tensor(t2[:, :w], psb[bank2][:, :w], sinT[:, l0:l0 + w], ALU.mult),
                         reads=[U('ps', bank2), U('tab')], writes=[U('tmp', t2i)])
                    S.op(ROPE_ENG, lambda e, t1=t1, t2=t2, t0=t0, w=w: e.tensor_tensor(dst_buf[:, t0:t0 + w], t1[:, :w], t2[:, :w], ALU.add),
                         reads=[U('tmp', t1i), U('tmp', t2i)], writes=[U(dst_unit, blk)])
                else:
                    S.op('act', lambda e, bank=bank, t0=t0, w=w: e.activation(dst_buf[:, t0:t0 + w], psb[bank][:, :w], AF.Copy),
                         reads=[U('ps', bank)], writes=[U(dst_unit, blk)])

        def project_v(wslot):
            wv = wr[wslot].rearrange("p (k j) -> p k j", k=8)
            for q4 in range(5):
                tcs = list(range(q4 * 4, min(18, q4 * 4 + 4)))
                bank = work_next()
                def mm(e, bank=bank, tcs=tcs):
                    r = None
                    for i, tc in enumerate(tcs):
                        for kc in range(8):
                            r = e.matmul(psb[bank][:, i * 128:(i + 1) * 128], xn[:, kc, tc * 128:(tc + 1) * 128], wv[:, kc, :],
                                         start=(kc == 0), stop=(kc == 7), skip_group_check=True)
                    return r
                blks = sorted(set([0 if tc < 2 else 1 + (tc - 2) // 4 for tc in tcs]))
                S.op('pe', mm, reads=[U('w', wslot)] + [U('xn', c, bk) for c in range(8) for bk in blks], writes=[U('ps', bank)])
                n = len(tcs)
                S.op('act', lambda e, bank=bank, q4=q4, n=n: e.activation(
                    vtok[:, q4 * 4:q4 * 4 + n, :], psb[bank][:, 0:n * 128].rearrange("p (t e) -> p t e", t=n), AF.Copy),
                    reads=[U('ps', bank)], writes=[U('v', q4)])

        def shift_bounds(qbuf, kbuf, qunit, kunit, qblocks, nbias):
            for a, (buf, unit, blocks) in enumerate([(qbuf, qunit, qblocks), (kbuf, kunit, [0, 1, 2, 3, 4])]):
                for blk in blocks:
                    t0, w = BLK[blk]
                    ti = tmp_next()
                    sq = tmpv(ti, BF16)
                    S.op('act', lambda e, sq=sq, buf=buf, t0=t0, w=w: e.activation(sq[:, :w], buf[:, t0:t0 + w], AF.Square),
                         reads=[U(unit, blk)], writes=[U('tmp', ti)])
                    for m in range(2):
                        bank = work_next()
                        S.op('pe', lambda e, sq=sq, m=m, bank=bank, w=w: e.matmul(
                            psb[bank][:, :w], ones_bf[m * 64:(m + 1) * 64, :], sq[m * 64:(m + 1) * 64, :w], start=True, stop=True),
                            reads=[U('tmp', ti), U('ones')], writes=[U('ps', bank)])
                        S.op('dve', lambda e, a=a, m=m, blk=blk, bank=bank, w=w: e.tensor_reduce(
                            mx[:, a, m, blk:blk + 1], psb[bank][:, :w], AX.X, ALU.max),
                            reads=[U('ps', bank)], writes=[U('mx', a, m, blk)])
            m2 = small[:, 8:12]
            qb0 = min(qblocks)
            S.op('dve', lambda e: e.tensor_reduce(m2[:, 0:2], mx[:, 0, :, qb0:5], AX.X, ALU.max),
                 reads=[U('mx', 0, m, blk) for m in range(2) for blk in qblocks], writes=[U('m2')])
            S.op('dve', lambda e: e.tensor_reduce(m2[:, 2:4], mx[:, 1, :, 0:5], AX.X, ALU.max),
                 reads=[U('mx', 1, m, blk) for m in range(2) for blk in range(5)] + [U('m2')], writes=[U('m2')])
            c2 = small[:, 12:14]
            S.op('dve', lambda e: e.tensor_tensor(c2, m2[:, 0:2], m2[:, 2:4], ALU.mult), reads=[U('m2')], writes=[U('c2')])
            S.op('act', lambda e: e.activation(c2, c2, AF.Ln), reads=[U('c2')], writes=[U('c2')])
            S.op('act', lambda e: e.activation(c2, c2, AF.Exp, scale=0.5), reads=[U('c2')], writes=[U('c2')])
            S.op('dve', lambda e: e.tensor_scalar(nbias, c2, -0.125, None, ALU.mult), reads=[U('c2')], writes=[U('nbias')])

        def wo_accumulate(l, wslot, b, blocks):
            for oc in range(8):
                for blk in blocks:
                    t0, w = BLK[blk]
                    j = 2 if blk == 0 else b
                    bank = work_next()
                    S.op('pe', lambda e, bank=bank, oc=oc, t0=t0, w=w: e.matmul(
                        psb[bank][:, :w], wr[wslot][:, oc * 128:(oc + 1) * 128], oT[:, t0:t0 + w], start=True, stop=True),
                        reads=[U('w', wslot), U('oT', blk)], writes=[U('ps', bank)])
                    st['wo'] = (st['wo'] + 1) % 3
                    if st['wo'] == 0:
                        ti = tmp_next()
                        tt = tmpv(ti, F32)
                        S.op('act', lambda e, tt=tt, bank=bank, oc=oc, w=w, j=j: e.activation(
                            tt[:, :w], psb[bank][:, :w], AF.Identity, scale=modT[:, l, 16 + oc, j:j + 1]),
                            reads=[U('ps', bank), U('modT')], writes=[U('tmp', ti)])
                        S.op('pool', lambda e, tt=tt, oc=oc, t0=t0, w=w: e.tensor_tensor(
                            hT[:, oc, t0:t0 + w], hT[:, oc, t0:t0 + w], tt[:, :w], ALU.add),
                            reads=[U('tmp', ti), U('h', oc, blk)], writes=[U('h', oc, blk)])
                    else:
                        S.op('dve', lambda e, bank=bank, oc=oc, t0=t0, w=w, j=j: e.scalar_tensor_tensor(
                            hT[:, oc, t0:t0 + w], psb[bank][:, :w], modT[:, l, 16 + oc, j:j + 1], hT[:, oc, t0:t0 + w], ALU.mult, ALU.add),
                            reads=[U('ps', bank), U('modT'), U('h', oc, blk)], writes=[U('h', oc, blk)])

        PAIRS = [(0, 1), (2, 3), (6, 7)]
        DACC_ENG = ['dve', 'pool']

        def pair_next():
            i = st['pw']
            st['pw'] = (i + 1) % len(PAIRS)
            return PAIRS[i]

        def run_pairs(pairs, LA=2):
            for p in pairs:
                p['S'] = freeze(p['S'])
                p['acc'] = freeze(p['acc'])
            n = len(pairs)
            for i in range(n + LA):
                if i < n:
                    p = pairs[i]
                    ba, bb = pair_next()
                    w = p['w']
                    S.op('pe', lambda e, p=p, ba=ba, bb=bb: p['S'](e, psb[ba], psb[bb]), reads=p['s_reads'],
                         writes=[U('ps', ba), U('ps', bb)])
                    pis = []
                    for hf, bank in enumerate((ba, bb)):
                        pi = pt_next()
                        pis.append(pi)
                        S.op('act', lambda e, p=p, bank=bank, pi=pi, w=w, hf=hf: e.activation(
                            PT[pi][:, :w], psb[bank][:, :w], AF.Exp, bias=p['bias'][hf], scale=0.125),
                            reads=[U('ps', bank), U('nbias')], writes=[U('pt', pi)])
                        if p.get('post') is not None:
                            p['post'](hf, PT[pi], pi)
                        eng = DACC_ENG[hf]
                        if p['first']:
                            S.op(eng, lambda e, pi=pi, hf=hf, w=w: e.tensor_copy(dacc[hf][:, :w], PT[pi][:, :w]),
                                 reads=[U('pt', pi)], writes=[U('dacc', hf)])
                        else:
                            S.op(eng, lambda e, pi=pi, hf=hf, w=w: e.tensor_tensor(dacc[hf][:, :w], dacc[hf][:, :w], PT[pi][:, :w], ALU.add),
                                 reads=[U('pt', pi), U('dacc', hf)], writes=[U('dacc', hf)])
                    p['pts'] = pis
                    if p.get('pre_done') is not None:
                        p['pre_done']()
                k = i - LA
                if 0 <= k < n:
                    p = pairs[k]
                    pa, pb = p['pts']
                    S.op('pe', lambda e, p=p, pa=pa, pb=pb: p['acc'](e, PT[pa], PT[pb]),
                         reads=[U('pt', pa), U('pt', pb)] + p['a_reads'], writes=[U('ps', 4), U('ps', 5)])
                    if p.get('done') is not None:
                        p['done']()

        def den_recip(hf, w, rows=slice(0, 128)):
            eng = DACC_ENG[hf]
            hi_i = tmp_next()
            hi = tmpv(hi_i, BF16)
            S.op(eng, lambda e: e.tensor_copy(hi[:, :w], dacc[hf][:, :w]), reads=[U('dacc', hf)], writes=[U('tmp', hi_i)])
            lo_i = tmp_next()
            lo = tmpv(lo_i, BF16)
            S.op(eng, lambda e: e.tensor_tensor(lo[:, :w], dacc[hf][:, :w], hi[:, :w], ALU.subtract),
                 reads=[U('dacc', hf), U('tmp', hi_i)], writes=[U('tmp', lo_i)])
            bank = work_next()
            def mm(e):
                e.matmul(psb[bank][:, :w], ones_bf[:], hi[:, :w], start=True, stop=False)
                return e.matmul(psb[bank][:, :w], ones_bf[:], lo[:, :w], start=False, stop=True)
            S.op('pe', mm, reads=[U('tmp', hi_i), U('tmp', lo_i), U('ones')], writes=[U('ps', bank)])
            ri = tmp_next()
            r = tmpv(ri, F32)
            S.op('act', lambda e: e.activation(r[rows, :w], psb[bank][rows, :w], AF.Ln), reads=[U('ps', bank)], writes=[U('tmp', ri)])
            S.op('act', lambda e: e.activation(r[rows, :w], r[rows, :w], AF.Exp, scale=-1.0), reads=[U('tmp', ri)], writes=[U('tmp', ri)])
            return ri

        def diff_attention_layer(l, b, need_ctx):
            qblocks = [0, 1, 2, 3, 4] if need_ctx else [1, 2, 3, 4]
            for g in range(8):
                par = g % 2
                qT_, kT_ = big[par * 2], big[par * 2 + 1]
                qU, kU = ('B', par * 2), ('B', par * 2 + 1)
                sl = w_alloc(4)
                load_w(wqkv_r[l, 0, g], sl[0])
                load_w(wqkv_r[l, 1, g], sl[1])
                load_w(wqkv_r[l, 2, g], sl[2])
                load_w(wo_r[l, g], sl[3])
                project_T(sl[0], qT_, qU, qblocks, rope=True)
                chk('projq')
                project_T(sl[1], kT_, kU, [0, 1, 2, 3, 4], rope=True)
                project_v(sl[2])
                chk('proj')
                nbias = small[:, 16 + par * 2:18 + par * 2]
                shift_bounds(qT_, kT_, qU, kU, qblocks, nbias)
                chk('bounds')
                pairs = []
                for qb in qblocks:
                    t0, w = BLK[qb]
                    kts = [0, 1] if qb == 0 else list(range(18))
                    for ki, kt in enumerate(kts):
                        first_k = (ki == 0)
                        last_k = (ki == len(kts) - 1)
                        kblk = 0 if kt < 2 else 1 + (kt - 2) // 4
                        def Sfn(e, ba, bb, kt=kt, t0=t0, w=w):
                            e.matmul(ba[:, :w], kT_[0:64, kt * 128:(kt + 1) * 128], qT_[0:64, t0:t0 + w], start=True, stop=True)
                            return e.matmul(bb[:, :w], kT_[64:128, kt * 128:(kt + 1) * 128], qT_[64:128, t0:t0 + w], start=True, stop=True)
                        def acc(e, pa, pb, kt=kt, w=w, first_k=first_k, last_k=last_k):
                            e.matmul(psb[4][:, :w], vtok[:, kt, :], pa[:, :w], start=first_k, stop=last_k)
                            return e.matmul(psb[5][:, :w], vtok[:, kt, :], pb[:, :w], start=first_k, stop=last_k)
                        pr = dict(S=Sfn, w=w, bias=[nbias[:, 0:1], nbias[:, 1:2]], acc=acc, first=first_k,
                                  s_reads=[U(kU[0], kU[1], kblk), U(qU[0], qU[1], qb)],
                                  a_reads=[U('v', kt // 4)])
                        if last_k:
                            rr = {}
                            def pre_done(w=w, rr=rr):
                                for m in range(2):
                                    rr[m] = den_recip(m, w)
                            pr['pre_done'] = pre_done
                            def done(qb=qb, t0=t0, w=w, rr=rr):
                                tis = []
                                for m in range(2):
                                    ri = rr[m]
                                    r = tmpv(ri, F32)
                                    ti = tmp_next()
                                    t = tmpv(ti, F32)
                                    S.op('dve', lambda e, t=t, r=r, m=m: e.tensor_tensor(t[:, :w], psb[4 + m][:, :w], r[:, :w], ALU.mult),
                                         reads=[U('ps', 4 + m), U('tmp', ri)], writes=[U('tmp', ti)])
                                    tis.append(ti)
                                t0i, t1i = tis
                                ta, tb = tmpv(t0i, F32), tmpv(t1i, F32)
                                S.op('dve', lambda e: e.scalar_tensor_tensor(ta[:, :w], tb[:, :w], negl, ta[:, :w], ALU.mult, ALU.add),
                                     reads=[U('tmp', t0i), U('tmp', t1i), U('negl')], writes=[U('tmp', t0i)])
                                si = tmp_next()
                                sq = tmpv(si, BF16)
                                S.op('act', lambda e: e.activation(sq[:, :w], ta[:, :w], AF.Square),
                                     reads=[U('tmp', t0i)], writes=[U('tmp', si)])
                                bank = work_next()
                                S.op('pe', lambda e: e.matmul(psb[bank][:, :w], ones_bf[:], sq[:, :w], start=True, stop=True),
                                     reads=[U('tmp', si), U('ones')], writes=[U('ps', bank)])
                                S.op('act', lambda e: e.activation(tb[:, :w], psb[bank][:, :w], AF.Ln, bias=eps_sb[:], scale=1.0 / 128),
                                     reads=[U('ps', bank), U('eps')], writes=[U('tmp', t1i)])
                                S.op('act', lambda e: e.activation(tb[:, :w], tb[:, :w], AF.Exp, scale=-0.5),
                                     reads=[U('tmp', t1i)], writes=[U('tmp', t1i)])
                                S.op('dve', lambda e: e.scalar_tensor_tensor(oT[:, t0:t0 + w], ta[:, :w], sublnS, tb[:, :w], ALU.mult, ALU.mult),
                                     reads=[U('tmp', t0i), U('tmp', t1i), U('sublnS')], writes=[U('oT', qb)])
                            pr['done'] = done
                        pairs.append(pr)
                run_pairs(pairs)
                chk('tiles')
                wo_accumulate(l, sl[3], b, qblocks)
                chk('head0')

        NAG = _na_groups()

        def na_tables(head, slot):
            for i in range(2):
                src = bass.AP(zscr_t, head * 15 * 127, [[1, 64], [127, 15], [1, 64]])
                S.op('sp', lambda e, i=i, src=src: e.dma_start(out=na_stg[i * 64:(i + 1) * 64, i + 3:i + 18, :], in_=src),
                     reads=[U('zscr')], writes=[U('nastg', i)], dma=True)
            full = na_tab[slot][0]
            intr = na_tab[slot][1]
            stg_rev = bass.AP(na_stg.tensor, na_stg.offset + 63, [list(na_stg.ap[0]), [64, 22], [-1, 64]])
            cm_rev = bass.AP(cmask.tensor, cmask.offset + 63, [list(cmask.ap[0]), [0, 22], [-1, 64]])
            S.op('dve', lambda e: e.tensor_tensor(full[:], stg_rev, cm_rev, ALU.mult),
                 reads=[U('nastg', 0), U('nastg', 1), U('cmask')], writes=[U('natab', slot, 0)])
            for i in range(2):
                S.op('dve', lambda e, i=i: e.tensor_copy(intr[i * 64:(i + 1) * 64, i + 7:i + 15, :], full[i * 64:(i + 1) * 64, i + 7:i + 15, :]),
                     reads=[U('natab', slot, 0)], writes=[U('natab', slot, 1, i)])

        def na_attention_layer(l, b):
            qblocks = [1, 2, 3, 4]
            for g in range(8):
                par = g % 2
                qT_, kT_ = big[par * 2], big[par * 2 + 1]
                qU, kU = ('B', par * 2), ('B', par * 2 + 1)
                sl = w_alloc(4)
                load_w(wqkv_r[l, 0, g], sl[0])
                load_w(wqkv_r[l, 1, g], sl[1])
                load_w(wqkv_r[l, 2, g], sl[2])
                load_w(wo_r[l, g], sl[3])
                project_T(sl[0], qT_, qU, qblocks, rope=False)
                project_T(sl[1], kT_, kU, [0, 1, 2, 3, 4], rope=False)
                project_v(sl[2])
                nbias = small[:, 16 + par * 2:18 + par * 2]
                shift_bounds(qT_, kT_, qU, kU, qblocks, nbias)
                for hh in range(2):
                    na_tables(2 * g + hh, hh)
                pairs = []
                for (r0, nr, kind, chs) in NAG:
                    w = nr * 64
                    t0 = LCTX + r0 * 64
                    qbs = sorted(set([1 + (r0 * 64) // 512, 1 + ((r0 + nr) * 64 - 1) // 512]))
                    klist = [('ctx', 0), ('ctx', 1)] + [('band', ch) for ch in chs]
                    for ki, (kk, ch) in enumerate(klist):
                        first_k = (ki == 0)
                        last_k = (ki == len(klist) - 1)
                        kt = ch if kk == 'ctx' else 2 + ch
                        kblk = 0 if kt < 2 else 1 + (kt - 2) // 4
                        def Sfn(e, ba, bb, kt=kt, t0=t0, w=w):
                            e.matmul(ba[:, :w], kT_[0:64, kt * 128:(kt + 1) * 128], qT_[0:64, t0:t0 + w], start=True, stop=True)
                            return e.matmul(bb[:, :w], kT_[64:128, kt * 128:(kt + 1) * 128], qT_[64:128, t0:t0 + w], start=True, stop=True)
                        def acc(e, pa, pb, kt=kt, w=w, first_k=first_k, last_k=last_k):
                            e.matmul(psb[4][:, :w], vtok[:, kt, :], pa[:, :w], start=first_k, stop=last_k)
                            return e.matmul(psb[5][:, :w], vtok[:, kt, :], pb[:, :w], start=first_k, stop=last_k)
                        pr = dict(S=Sfn, w=w, bias=[nbias[:, 0:1], nbias[:, 1:2]], acc=acc, first=first_k,
                                  s_reads=[U(kU[0], kU[1], kblk)] + [U(qU[0], qU[1], qb) for qb in qbs],
                                  a_reads=[U('v', kt // 4)])
                        if kk == 'band':
                            Dd = 2 * ch - r0
                            u0 = 7 - Dd + 3
                            assert 0 <= u0 and u0 + nr <= 22, (u0, nr)
                            def post(hf, pt, pi, u0=u0, nr=nr, w=w, kind=kind):
                                tabv = na_tab[hf][0 if kind == 'full' else 1]
                                rd = [U('natab', hf, 0)] if kind == 'full' else [U('natab', hf, 1, 0), U('natab', hf, 1, 1)]
                                S.op('dve', lambda e: e.tensor_tensor(pt[:, :w], pt[:, :w],
                                                                      tabv[:, u0:u0 + nr, :].rearrange("p u c -> p (u c)"), ALU.mult),
                                     reads=[U('pt', pi)] + rd, writes=[U('pt', pi)])
                            pr['post'] = post
                        if last_k:
                            rr = {}
                            def pre_done(w=w, rr=rr):
                                for hh in range(2):
                                    rr[hh] = den_recip(hh, w, rows=slice(hh * 64, (hh + 1) * 64))
                            pr['pre_done'] = pre_done
                            def done(t0=t0, w=w, qbs=qbs, rr=rr):
                                for hh in range(2):
                                    hs = slice(hh * 64, (hh + 1) * 64)
                                    ri = rr[hh]
                                    r = tmpv(ri, F32)
                                    S.op('dve', lambda e, hh=hh, hs=hs, r=r: e.tensor_tensor(oT[hs, t0:t0 + w], psb[4 + hh][hs, :w], r[hs, :w], ALU.mult),
                                         reads=[U('ps', 4 + hh), U('tmp', ri)] + [U('oT', qb) for qb in qbs],
                                         writes=[U('oT', qb) for qb in qbs])
                            pr['done'] = done
                        pairs.append(pr)
                run_pairs(pairs)
                wo_accumulate(l, sl[3], b, qblocks)

        def ffn_layer(l, b, blocks):
            norm_modulate(l, 1, b, blocks)
            JG = 4
            for j0 in range(0, NJ, JG):
                js = list(range(j0, min(NJ, j0 + JG)))
                wslots = {}
                for jj, j in enumerate(js):
                    sl = w_alloc(3)
                    wslots[j] = sl
                    S.op('pool', lambda e, sl=sl, j=j: e.dma_start(
                        out=wring[:, sl[0] * 1024:(sl[0] + 2) * 1024], in_=w1_r[l, j]),
                        writes=[U('w', sl[0]), U('w', sl[1])], dma=True)
                    load_w(w2_r[l, j], sl[2])
                    wg = wr[sl[0]].rearrange("p (k j) -> p k j", k=8)
                    wu = wr[sl[1]].rearrange("p (k j) -> p k j", k=8)
                    for blk in blocks:
                        t0, w = BLK[blk]
                        bg = work_next()
                        bu = work_next()
                        def mm(e, wg=wg, wu=wu, bg=bg, bu=bu, t0=t0, w=w):
                            for kc in range(8):
                                e.matmul(psb[bg][:, :w], wg[:, kc, :], xn[:, kc, t0:t0 + w], start=(kc == 0), stop=(kc == 7))
                            for kc in range(8):
                                r = e.matmul(psb[bu][:, :w], wu[:, kc, :], xn[:, kc, t0:t0 + w], start=(kc == 0), stop=(kc == 7))
                            return r
                        S.op('pe', mm, reads=[U('w', sl[0]), U('w', sl[1])] + [U('xn', c, blk) for c in range(8)],
                             writes=[U('ps', bg), U('ps', bu)])
                        ti = tmp_next()
                        sg = tmpv(ti, F32)
                        S.op('act', lambda e, sg=sg, bg=bg, w=w: e.activation(sg[:, :w], psb[bg][:, :w], AF.Silu),
                             reads=[U('ps', bg)], writes=[U('tmp', ti)])
                        S.op('dve', lambda e, sg=sg, bu=bu, jj=jj, t0=t0, w=w: e.tensor_tensor(big[jj][:, t0:t0 + w], sg[:, :w], psb[bu][:, :w], ALU.mult),
                             reads=[U('tmp', ti), U('ps', bu)], writes=[U('B', jj, blk)])
                for oc in range(8):
                    for blk in blocks:
                        t0, w = BLK[blk]
                        jv = 2 if blk == 0 else b
                        bank = work_next()
                        def mm2(e, bank=bank, oc=oc, t0=t0, w=w):
                            for jj, j in enumerate(js):
                                r = e.matmul(psb[bank][:, :w], wr[wslots[j][2]][:, oc * 128:(oc + 1) * 128], big[jj][:, t0:t0 + w],
                                             start=(jj == 0), stop=(jj == len(js) - 1))
                            return r
                        S.op('pe', mm2, reads=[U('w', wslots[j][2]) for j in js] + [U('B', jj, blk) for jj in range(len(js))],
                             writes=[U('ps', bank)])
                        S.op('dve', lambda e, bank=bank, oc=oc, t0=t0, w=w, jv=jv: e.scalar_tensor_tensor(
                            hT[:, oc, t0:t0 + w], psb[bank][:, :w], modT[:, l, 40 + oc, jv:jv + 1], hT[:, oc, t0:t0 + w], ALU.mult, ALU.add),
                            reads=[U('ps', bank), U('modT'), U('h', oc, blk)], writes=[U('h', oc, blk)])

        for b in range(nb):
            for c in range(8):
                for blk in range(5):
                    pass
                S.op('sp', lambda e, b=b, c=c: e.dma_start(out=hT[:, c, :], in_=xT[b, c * 128:(c + 1) * 128, :]),
                     writes=[U('h', c, blk) for blk in range(5)], dma=True)
            for l in layers:
              try:
                last = (l == 1)
                blocks = [1, 2, 3, 4] if last else [0, 1, 2, 3, 4]
                if stop == 'setup':
                    break
                norm_modulate(l, 0, b, [0, 1, 2, 3, 4])
                if stop == 'norm':
                    break
                if l == 0:
                    S.op('sp', lambda e: e.dma_start(out=tab[:].bitcast(F32).rearrange("p (a n) -> p a n", a=2), in_=ropet),
                         writes=[U('tab')] + [U('natab', s, 0) for s in range(2)] +
                                [U('natab', s, 1, i) for s in range(2) for i in range(2)] + [U('nastg', i) for i in range(2)], dma=True)
                    diff_attention_layer(l, b, need_ctx=not last)
                else:
                    S.op('pool', lambda e: e.memset(tab[:].bitcast(BF16), 0.0), reads=[U('tab')],
                         writes=[U('tab')] + [U('nastg', i) for i in range(2)] + [U('natab', s, 0) for s in range(2)] +
                                [U('natab', s, 1, i) for s in range(2) for i in range(2)])
                    na_attention_layer(l, b)
                if stop == 'attn':
                    break
                ffn_layer(l, b, blocks)
              except _Stop:
                break
            if final:
                for blk in [1, 2, 3, 4]:
                    t0, w = BLK[blk]
                    ri = rms_rstd(None, blk, 1.0 / D)
                    r = rstd_buf[ri]
                    for c in range(8):
                        ti = tmp_next()
                        t = tmpv(ti, F32)
                        S.op('dve', lambda e, t=t, c=c, t0=t0, w=w, r=r: e.scalar_tensor_tensor(
                            t[:, :w], hT[:, c, t0:t0 + w], nrm_sb[:, 4, c:c + 1], r[:, :w], ALU.mult, ALU.mult),
                            reads=[U('h', c, blk), U('rstd', ri), U('nrm')], writes=[U('tmp', ti)])
                        S.op('sp', lambda e, t=t, c=c, b=b, t0=t0, w=w: e.dma_start(
                            out=outT[b, c * 128:(c + 1) * 128, t0 - LCTX:t0 - LCTX + w], in_=t[:, :w]),
                            reads=[U('tmp', ti)], dma=True)
            else:
                for c in range(8):
                    S.op('sp', lambda e, b=b, c=c: e.dma_start(out=outT[b, c * 128:(c + 1) * 128, :], in_=hT[:, c, :]),
                         reads=[U('h', c, blk) for blk in range(5)], dma=True)
        S.emit()
    return nc


def _vecT(v):
    v = np.asarray(v, np.float32)
    return np.ascontiguousarray(np.moveaxis(v.reshape(v.shape[:-1] + (8, 128)), -1, 0))


def prepare_shared(inputs):
    f = lambda k: np.asarray(inputs[k], np.float32)
    sh = {}
    ada_w = f("ada_w")
    sh["ada_r"] = np.ascontiguousarray(ada_w.reshape(2, 8, 128, 6, 1024).transpose(0, 3, 1, 2, 4))
    sh["ada_bT"] = np.ascontiguousarray(f("ada_b").reshape(2, 48, 128).transpose(2, 0, 1))
    nrm = np.stack([f("norm_mix")[0], f("norm_mix")[1], f("norm_ffn")[0], f("norm_ffn")[1], f("norm_final")], 0)
    sh["nrm"] = np.ascontiguousarray(nrm.reshape(5, 8, 128).transpose(2, 0, 1))
    wqkv = np.stack([f("da_wqkv")[0], f("na_wqkv")[0]], 0)
    sh["wqkv_r"] = np.ascontiguousarray(wqkv.reshape(2, 8, 128, 3, 8, 128).transpose(0, 3, 4, 2, 1, 5)).reshape(2, 3, 8, 128, 1024)
    wo = np.stack([f("da_wo")[0], f("na_wo")[0]], 0)
    sh["wo_r"] = np.ascontiguousarray(wo.reshape(2, 8, 128, 1024))
    w1 = f("ffn_w_gate_up")
    sh["w1_r"] = np.ascontiguousarray(w1.reshape(2, 8, 128, 2, NJ, 128).transpose(0, 4, 2, 3, 1, 5)).reshape(2, NJ, 128, 2048)
    sh["w2_r"] = np.ascontiguousarray(f("ffn_w_down").reshape(2, NJ, 128, 1024))
    sh["lamv"] = np.ascontiguousarray(np.concatenate([f("da_lambda_q1")[0], f("da_lambda_k1")[0],
                                                      f("da_lambda_q2")[0], f("da_lambda_k2")[0]])[None, :])
    sh["subln"] = np.ascontiguousarray(f("da_subln")[0][:, None])
    rp = f("na_rpb")[0][:, ::-1, :]
    sh["rpb"] = np.ascontiguousarray(rp.reshape(120, 2, 31))
    sh["ropet"] = _rope_tables()
    sh["rotm"] = _rot_matrix()
    sh["cmask"] = _na_colmask_rev()
    return sh


def prepare_core(inputs, core, nb=NB):
    x = np.asarray(inputs["x"], np.float32)
    ctx = np.asarray(inputs["ctx"], np.float32)
    c = np.asarray(inputs["c"], np.float32)
    c_ctx = np.asarray(inputs["c_ctx"], np.float32)
    bs = slice(core * nb, (core + 1) * nb)
    xc = np.concatenate([ctx[bs], x[bs]], axis=1)
    m = {"xT": np.ascontiguousarray(xc.transpose(0, 2, 1))}
    cols = [c[core * nb + i] for i in range(nb)]
    while len(cols) < 2:
        cols.append(cols[-1])
    cols.append(c_ctx)
    cm = np.stack(cols, axis=-1)
    m["cT"] = np.ascontiguousarray(cm.reshape(8, 128, 3).transpose(1, 0, 2))
    return m


_PROG = {}


def kernel(**inputs):
    if 'full' not in _PROG:
        _PROG['full'] = build_program()
    nc = _PROG['full']
    sh = prepare_shared(inputs)
    in_maps = []
    for core in range(NCORES):
        m = dict(sh)
        m.update(prepare_core(inputs, core))
        in_maps.append(m)
    res = run_bass_kernel_spmd(nc, in_maps, core_ids=list(range(NCORES)))
    outs = [np.asarray(r["outT"]) for r in res.results]
    o = np.concatenate(outs, axis=0)
    return np.ascontiguousarray(o.transpose(0, 2, 1)).astype(np.float32)
```
